# Optimizing a Trainium2 kernel written in Bass

```python
import math
import jax, jax.numpy as jnp
from jax import lax
import numpy as np

D_MODEL = 1024
BATCH = 16
SEQ = 2048
DEPTH = 4

N_MEM = 256
D_MIX = D_MODEL
D_SSM = D_MIX // 2
SSM_GROUP = 16
N_SSM_GROUPS = D_SSM // SSM_GROUP
SSM_STATE = 64
N_DIR = 2
D_GMLP = D_MIX - D_SSM
GMLP_CHUNK = 128
GMLP_HEAD = 128
N_GMLP_HEADS = D_GMLP // GMLP_HEAD
D_IN = D_SSM + 2 * D_GMLP
N_XATTN_HEADS = 4
XATTN_HEAD_DIM = D_MODEL // N_XATTN_HEADS
D_FF = 256 * ((8 * D_MODEL // 3 + 255) // 256)
CONV_WIDTH = 3
RMS_EPS = 1e-6
DT_MIN = 1e-3
DT_MAX = 1e-1

kernel_name = "hybrid_s5_gmlp_memory_encoder"


def rmsnorm(x, g):
    xf = x.astype(jnp.float32)
    y = xf * lax.rsqrt(jnp.mean(xf * xf, axis=-1, keepdims=True) + RMS_EPS)
    return (y * g.astype(jnp.float32)).astype(x.dtype)


def s5_direction(u_t, a_re, a_im, log_dt, b_re, b_im, c_re, c_im, reverse):
    a_re = a_re.astype(jnp.float32)
    a_im = a_im.astype(jnp.float32)
    dt = jnp.exp(log_dt.astype(jnp.float32))[:, None]
    mag = jnp.exp(a_re * dt)
    lb_re = mag * jnp.cos(a_im * dt)
    lb_im = mag * jnp.sin(a_im * dt)
    den = a_re * a_re + a_im * a_im
    n_re = lb_re - 1.0
    n_im = lb_im
    f_re = (n_re * a_re + n_im * a_im) / den
    f_im = (n_im * a_re - n_re * a_im) / den
    b_re = b_re.astype(jnp.float32)
    b_im = b_im.astype(jnp.float32)
    bb_re = f_re[..., None] * b_re - f_im[..., None] * b_im
    bb_im = f_re[..., None] * b_im + f_im[..., None] * b_re
    uf = u_t.astype(jnp.float32)
    bu_re = jnp.einsum('lbgh,gph->lbgp', uf, bb_re)
    bu_im = jnp.einsum('lbgh,gph->lbgp', uf, bb_im)
    n_pos = u_t.shape[0]
    lam_re = jnp.broadcast_to(lb_re[None, None], (n_pos, 1) + lb_re.shape)
    lam_im = jnp.broadcast_to(lb_im[None, None], (n_pos, 1) + lb_im.shape)

    def combine(left, right):
        a1r, a1i, b1r, b1i = left
        a2r, a2i, b2r, b2i = right
        ar = a2r * a1r - a2i * a1i
        ai = a2r * a1i + a2i * a1r
        br = a2r * b1r - a2i * b1i + b2r
        bi = a2r * b1i + a2i * b1r + b2i
        return (ar, ai, br, bi)

    _, _, x_re, x_im = lax.associative_scan(
        combine, (lam_re, lam_im, bu_re, bu_im), axis=0, reverse=reverse)
    return (jnp.einsum('lbgp,ghp->lbgh', x_re, c_re.astype(jnp.float32))
            - jnp.einsum('lbgp,ghp->lbgh', x_im, c_im.astype(jnp.float32)))


def s5_mixer(u, a_re, a_im, log_dt, b_re, b_im, c_re, c_im, d_skip, w_glu, b_glu):
    bsz, n_pos, _ = u.shape
    u_t = u.reshape(bsz, n_pos, N_SSM_GROUPS, SSM_GROUP).transpose(1, 0, 2, 3)
    y = (s5_direction(u_t, a_re[0], a_im[0], log_dt[0], b_re[0], b_im[0], c_re[0], c_im[0], False)
         + s5_direction(u_t, a_re[1], a_im[1], log_dt[1], b_re[1], b_im[1], c_re[1], c_im[1], True))
    y = y.transpose(1, 0, 2, 3).reshape(bsz, n_pos, D_SSM)
    y = jax.nn.gelu(y + d_skip.astype(jnp.float32) * u.astype(jnp.float32))
    y = y.astype(u.dtype)
    return y * jax.nn.sigmoid(y @ w_glu + b_glu)


def gmlp_mixer(u, v, g_v, w_s, b_s):
    bsz, n_pos, _ = u.shape
    u = jax.nn.gelu(u)
    v = rmsnorm(jax.nn.gelu(v), g_v)
    vc = v.reshape(bsz, n_pos // GMLP_CHUNK, GMLP_CHUNK, N_GMLP_HEADS, GMLP_HEAD)
    s = jnp.einsum('hqk,bnkhc->bnqhc', w_s, vc) + b_s.T[None, None, :, :, None]
    return u * s.reshape(bsz, n_pos, D_GMLP)


def memory_cross_attention(z, mem, g_mem, w_q, w_kv, w_o):
    bsz, n_pos, _ = z.shape
    q = (z @ w_q).reshape(bsz, n_pos, N_XATTN_HEADS, XATTN_HEAD_DIM)
    kv = rmsnorm(mem, g_mem) @ w_kv
    k, v = jnp.split(kv, 2, axis=-1)
    k = k.reshape(bsz, -1, N_XATTN_HEADS, XATTN_HEAD_DIM)
    v = v.reshape(bsz, -1, N_XATTN_HEADS, XATTN_HEAD_DIM)
    scores = jnp.einsum('blhd,bmhd->bhlm', q, k).astype(jnp.float32) * (XATTN_HEAD_DIM ** -0.5)
    p = jax.nn.softmax(scores, axis=-1).astype(v.dtype)
    o = jnp.einsum('bhlm,bmhd->blhd', p, v).reshape(bsz, n_pos, D_MODEL)
    return o @ w_o


def conv_ffn(z, w_up, conv_w, conv_b, w_down):
    hdn = z @ w_up
    hp = jnp.pad(hdn, ((0, 0), (1, 1), (0, 0)))
    hdn = conv_w[0] * hp[:, :-2] + conv_w[1] * hp[:, 1:-1] + conv_w[2] * hp[:, 2:] + conv_b
    gate, val = jnp.split(hdn, 2, axis=-1)
    return (jax.nn.gelu(gate) * val) @ w_down


def setup_inputs(seed: int = 0) -> dict:
    key = jax.random.key(seed)
    ks = jax.random.split(key, 32)

    def nrm(k, shape, scale):
        return jax.random.normal(k, shape, jnp.float32) * scale

    G, P, H = N_SSM_GROUPS, SSM_STATE, SSM_GROUP
    n_idx = jnp.arange(P, dtype=jnp.float32)
    a_re = -0.5 + nrm(ks[4], (DEPTH, N_DIR, G, P), 0.01)
    a_im = math.pi * n_idx + nrm(ks[5], (DEPTH, N_DIR, G, P), 0.01)
    log_dt = jax.random.uniform(ks[6], (DEPTH, N_DIR, G), jnp.float32,
                                math.log(DT_MIN), math.log(DT_MAX))
    return {
        "x": nrm(ks[0], (BATCH, SEQ, D_MODEL), 1.0),
        "mem": nrm(ks[1], (BATCH, N_MEM, D_MODEL), 1.0),
        "norm_mix_g": 1.0 + nrm(ks[2], (DEPTH, D_MODEL), 0.02),
        "w_in": nrm(ks[3], (DEPTH, D_MODEL, D_IN), D_MODEL ** -0.5),
        "ssm_a_re": a_re,
        "ssm_a_im": a_im,
        "ssm_log_dt": log_dt,
        "ssm_b_re": nrm(ks[7], (DEPTH, N_DIR, G, P, H), (2.0 * H) ** -0.5),
        "ssm_b_im": nrm(ks[8], (DEPTH, N_DIR, G, P, H), (2.0 * H) ** -0.5),
        "ssm_c_re": nrm(ks[9], (DEPTH, N_DIR, G, H, P), P ** -0.5),
        "ssm_c_im": nrm(ks[10], (DEPTH, N_DIR, G, H, P), P ** -0.5),
        "ssm_d": nrm(ks[11], (DEPTH, D_SSM), 1.0),
        "w_glu": nrm(ks[12], (DEPTH, D_SSM, D_SSM), D_SSM ** -0.5),
        "b_glu": nrm(ks[13], (DEPTH, D_SSM), 0.02),
        "gmlp_norm_g": 1.0 + nrm(ks[14], (DEPTH, D_GMLP), 0.02),
        "gmlp_w_s": nrm(ks[15], (DEPTH, N_GMLP_HEADS, GMLP_CHUNK, GMLP_CHUNK), 0.5 * GMLP_CHUNK ** -0.5),
        "gmlp_b_s": 1.0 + nrm(ks[16], (DEPTH, N_GMLP_HEADS, GMLP_CHUNK), 0.02),
        "w_out": nrm(ks[17], (DEPTH, D_MIX, D_MODEL), D_MIX ** -0.5),
        "norm_xattn_g": 1.0 + nrm(ks[18], (DEPTH, D_MODEL), 0.02),
        "mem_norm_g": 1.0 + nrm(ks[19], (DEPTH, D_MODEL), 0.02),
        "w_q": nrm(ks[20], (DEPTH, D_MODEL, D_MODEL), D_MODEL ** -0.5),
        "w_kv": nrm(ks[21], (DEPTH, D_MODEL, 2 * D_MODEL), D_MODEL ** -0.5),
        "w_o": nrm(ks[22], (DEPTH, D_MODEL, D_MODEL), D_MODEL ** -0.5),
        "norm_ffn_g": 1.0 + nrm(ks[23], (DEPTH, D_MODEL), 0.02),
        "w_up": nrm(ks[24], (DEPTH, D_MODEL, 2 * D_FF), D_MODEL ** -0.5),
        "conv_w": nrm(ks[25], (DEPTH, CONV_WIDTH, 2 * D_FF), CONV_WIDTH ** -0.5),
        "conv_b": nrm(ks[26], (DEPTH, 2 * D_FF), 0.02),
        "w_down": nrm(ks[27], (DEPTH, D_FF, D_MODEL), D_FF ** -0.5),
        "final_g": 1.0 + nrm(ks[28], (D_MODEL,), 0.02),
    }


def reference(x, mem, norm_mix_g, w_in, ssm_a_re, ssm_a_im, ssm_log_dt, ssm_b_re, ssm_b_im,
              ssm_c_re, ssm_c_im, ssm_d, w_glu, b_glu, gmlp_norm_g, gmlp_w_s, gmlp_b_s, w_out,
              norm_xattn_g, mem_norm_g, w_q, w_kv, w_o, norm_ffn_g, w_up, conv_w, conv_b,
              w_down, final_g):
    h = x
    for i in range(DEPTH):
        z = rmsnorm(h, norm_mix_g[i])
        proj = z @ w_in[i]
        u_ssm = proj[..., :D_SSM]
        u_g = proj[..., D_SSM:D_SSM + D_GMLP]
        v_g = proj[..., D_SSM + D_GMLP:]
        y_ssm = s5_mixer(u_ssm, ssm_a_re[i], ssm_a_im[i], ssm_log_dt[i], ssm_b_re[i], ssm_b_im[i],
                         ssm_c_re[i], ssm_c_im[i], ssm_d[i], w_glu[i], b_glu[i])
        y_g = gmlp_mixer(u_g, v_g, gmlp_norm_g[i], gmlp_w_s[i], gmlp_b_s[i])
        h = h + jnp.concatenate([y_ssm, y_g], axis=-1) @ w_out[i]
        z = rmsnorm(h, norm_xattn_g[i])
        h = h + memory_cross_attention(z, mem, mem_norm_g[i], w_q[i], w_kv[i], w_o[i])
        z = rmsnorm(h, norm_ffn_g[i])
        h = h + conv_ffn(z, w_up[i], conv_w[i], conv_b[i], w_down[i])
    return rmsnorm(h, final_g)
```

```python
import math
import numpy as np
import concourse.bass as bass
import concourse.mybir as mybir
from concourse.bass_utils import run_bass_kernel_spmd
from concourse.ap import AP

F32 = mybir.dt.float32
BF16 = mybir.dt.bfloat16
ALU = mybir.AluOpType
AF = mybir.ActivationFunctionType
AX = mybir.AxisListType

D_MODEL = 1024
SEQ = 2048
N_MEM = 256
D_FF = 2816
NFF = 22
DEPTH = 4
EPS = 1e-6
NCORES = 8
ENGS = ("pe", "act", "dve", "pool", "sp")


class Reg:
    __slots__ = ("name", "w", "rs")

    def __init__(self, name=""):
        self.name = name
        self.w = None
        self.rs = []


class Prog:
    def __init__(self, nc):
        self.nc = nc
        self.ops = {e: [] for e in ENGS}
        self.cnt = {e: 0 for e in ENGS}
        self.seen = {e: {} for e in ENGS}
        self.sems = {}
        self.dma_cnt = {}
        self._ctx = []
        for e in ENGS:
            self._sem("eng_" + e)

    def _sem(self, key):
        if key not in self.sems:
            cm = self.nc.semaphore(key)
            h = cm.__enter__()
            self._ctx.append(cm)
            self.sems[key] = h
        return self.sems[key]

    def _deps(self, eng, reads, writes, skip_self):
        toks = []
        for r in reads:
            if r.w is not None:
                toks.append(r.w)
        for w in writes:
            if w.w is not None:
                toks.append(w.w)
            toks.extend(w.rs)
        waits = {}
        own = "eng_" + eng
        for (k, v) in toks:
            if k == own and (skip_self or v > self.cnt[eng]):
                continue
            if self.seen[eng].get(k, 0) >= v:
                continue
            waits[k] = max(waits.get(k, 0), v)
        for k, v in waits.items():
            self.seen[eng][k] = v
        return list(waits.items())

    def op(self, eng, fn, reads=(), writes=(), milestone=True, skip_self=False):
        waits = self._deps(eng, reads, writes, skip_self)
        own = "eng_" + eng
        if milestone:
            self.cnt[eng] += 1
        tok = (own, self.cnt[eng] if milestone else self.cnt[eng] + 1)
        self.ops[eng].append((waits, fn, (own, 1) if milestone else None))
        for r in reads:
            r.rs.append(tok)
            if len(r.rs) > 64:
                r.rs = _prune(r.rs)
        for w in writes:
            w.w = tok
            w.rs = []
        return tok

    def dma(self, eng, fn, semkey, reads=(), writes=(), n=1):
        self._sem(semkey)
        waits = self._deps(eng, reads, writes, False)
        self.dma_cnt[semkey] = self.dma_cnt.get(semkey, 0) + 16 * n
        tok = (semkey, self.dma_cnt[semkey])
        self.ops[eng].append((waits, fn, (semkey, 16)))
        for r in reads:
            r.rs.append(tok)
            if len(r.rs) > 64:
                r.rs = _prune(r.rs)
        for w in writes:
            w.w = tok
            w.rs = []
        return tok

    def wait_all(self, eng, regs):
        toks = []
        for r in regs:
            if r.w is not None:
                toks.append(r.w)
            toks.extend(r.rs)
        waits = {}
        for (k, v) in toks:
            if self.seen[eng].get(k, 0) >= v:
                continue
            waits[k] = max(waits.get(k, 0), v)
        for k, v in waits.items():
            self.seen[eng][k] = v
        self.ops[eng].append((list(waits.items()), None, None))

    def emit(self):
        nc = self.nc
        with nc.Block() as block:
            def mk(e):
                def body(engobj):
                    for (waits, fn, inc) in self.ops[e]:
                        for (k, v) in waits:
                            engobj.wait_ge(self.sems[k], v)
                        if fn is None:
                            continue
                        ins = fn(engobj)
                        if inc is not None:
                            if isinstance(ins, (list, tuple)):
                                for i in ins:
                                    i.then_inc(self.sems[inc[0]], inc[1])
                            else:
                                ins.then_inc(self.sems[inc[0]], inc[1])
                return body
            block.tensor(mk("pe"))
            block.scalar(mk("act"))
            block.vector(mk("dve"))
            block.gpsimd(mk("pool"))
            block.sync(mk("sp"))

    def close(self):
        for cm in reversed(self._ctx):
            cm.__exit__(None, None, None)


def _prune(toks):
    best = {}
    for (k, v) in toks:
        if best.get(k, 0) < v:
            best[k] = v
    return list(best.items())


def cap(base, off, dims):
    return AP(base.tensor, base.offset + off, [list(base.ap[0])] + [list(d) for d in dims])


class Cfg:
    def __init__(self, depth=DEPTH, nseq=2, mixer=True, xattn=True, ffn=True, dumps=()):
        self.depth = depth
        self.nseq = nseq
        self.mixer = mixer
        self.xattn = xattn
        self.ffn = ffn
        self.dumps = tuple(dumps)


def colp_layout(depth):
    off = {}
    n = 0
    for l in range(depth):
        for nm, w in (("g_mix", 8), ("g_x", 8), ("g_f", 8), ("ssm_d", 4), ("b_glu", 4),
                      ("cw0", 44), ("cw1", 44), ("cw2", 44), ("cb", 44)):
            off[(nm, l)] = n
            n += w
    return off, n


class Builder:
    def __init__(self, nc, cfg):
        self.nc = nc
        self.cfg = cfg
        self.P = Prog(nc)
        self.dump_out = {}
        self._uid = 0
        self.declare_io()
        self.alloc()

    def sb(self, name, cols, dtype):
        return self.nc.alloc_sbuf_tensor(name, [128, cols], dtype)

    def declare_io(self):
        nc, cfg = self.nc, self.cfg
        d = cfg.depth
        di = lambda name, shape: nc.dram_tensor(name, list(shape), F32, kind="ExternalInput").ap()
        self.x = di("x", (cfg.nseq, SEQ, D_MODEL))
        self.mem = di("mem", (cfg.nseq, N_MEM, D_MODEL))
        self.colp_off, ncol = colp_layout(d)
        self.colp_d = di("colp", (128, ncol))
        self.final_g = di("final_g", (1, D_MODEL))
        self.ident_d = di("ident", (128, 128))
        self.w_up = di("w_up", (d, D_MODEL, 2 * D_FF))
        self.w_down = di("w_down", (d, D_FF, D_MODEL))
        self.w_q = di("w_q", (d, D_MODEL, D_MODEL))
        self.w_kv = di("w_kv", (d, D_MODEL, 2 * D_MODEL))
        self.w_o = di("w_o", (d, D_MODEL, D_MODEL))
        self.mem_g = di("mem_g", (d, D_MODEL))
        self.w_in = di("w_in", (d, D_MODEL, 1536))
        self.w_out = di("w_out", (d, D_MODEL, D_MODEL))
        self.gv = di("gv", (d, 512))
        self.bs = di("bs", (d, 512))
        self.wsT = di("wsT", (d, 128, 512))
        self.w_glu = di("w_glu", (d, 512, 512))
        self.s5p = di("s5p", (d, 128, 96))
        self.s5b = di("s5b", (d, 128, 1024))
        self.s5c = di("s5c", (d, 128, 1024))
        self.s5k = di("s5k", (128, 1024))
        self.selb_d = di("selb", (128, 8 * 240))
        knd = "ExternalOutput" if "s5" in cfg.dumps else "Internal"
        self.s5w_d = nc.dram_tensor("s5w_scr", [d, 4, 128, 5120], BF16, kind=knd).ap()
        self.tab_d = nc.dram_tensor("s5t_scr", [d, 8, 128, 3072], F32, kind=knd).ap()
        self.tabb_d = nc.dram_tensor("s5tb_scr", [d, 8, 128, 2048], BF16).ap()
        self.rho_d = nc.dram_tensor("s5rho_scr", [d, 8, 128, 1024], F32).ap()
        self.out = nc.dram_tensor("out", [cfg.nseq, SEQ, D_MODEL], F32, kind="ExternalOutput").ap()

    def alloc(self):
        nc = self.nc
        self.H = self.sb("H", 8 * SEQ, F32)
        self.rH = [Reg("H%d" % c) for c in range(8)]
        self.A = self.sb("A", 8 * SEQ, BF16)
        self.rA = [Reg("A%d" % c) for c in range(8)]
        self.B = self.sb("B", 8 * SEQ, BF16)
        self.rB = [Reg("B%d" % c) for c in range(8)]
        self.NR = 3
        self.ring = [self.sb("ring%d" % i, 2048, BF16) for i in range(self.NR)]
        self.rring = [Reg("ring%d" % i) for i in range(self.NR)]
        self.ring_n = 0
        self.NPG = 14
        self.arena = self.sb("arena", 1024 * self.NPG, F32)
        self.rpg = [Reg("pg%d" % i) for i in range(self.NPG)]
        self.colp = self.sb("colp_sb", self.colp_d.shape[1], F32)
        self.r_colp = Reg("colp")
        self.ident = self.sb("ident_sb", 128, F32)
        self.r_ident = Reg("ident")
        self.ones_bf = self.sb("ones_bf", 128, BF16)
        self.r_ones = Reg("ones")
        self.ident_bf = self.sb("ident_bf", 128, BF16)
        self.selb = self.sb("selb_sb", 8 * 240, BF16)
        self.r_selb = Reg("selb")
        self.small = self.sb("small", 80, F32)
        self.r_small = Reg("small")
        self.ps = nc.alloc_psum_tensor("ps", [128, 4096], F32)
        self.rps = [Reg("ps%d" % i) for i in range(8)]

    def pgv(self, p0, n=1, dtype=F32):
        a = self.arena.ap()[:, p0 * 1024:(p0 + n) * 1024]
        return a if dtype == F32 else a.bitcast(dtype)

    def prs(self, p0, n=1):
        return self.rpg[p0:p0 + n]

    def hview(self, c, lo=0, hi=SEQ):
        return self.H.ap()[:, c * SEQ + lo: c * SEQ + hi]

    def aview(self, c, lo=0, hi=SEQ):
        return self.A.ap()[:, c * SEQ + lo: c * SEQ + hi]

    def bview(self, c, lo=0, hi=SEQ):
        return self.B.ap()[:, c * SEQ + lo: c * SEQ + hi]

    def psv(self, lo, hi):
        return self.ps.ap()[:, lo:hi]

    def col(self, name, l, j=0, n=1):
        o = self.colp_off[(name, l)] + j
        return self.colp.ap()[:, o:o + n]

    def mm(self, out, lhsT, rhs, start, stop, reads, writes, last):
        self.P.op("pe", lambda e: e.matmul(out, lhsT=lhsT, rhs=rhs, start=start, stop=stop),
                  reads=reads, writes=writes, milestone=last, skip_self=True)

    def ring_next(self):
        i = self.ring_n % self.NR
        self.ring_n += 1
        return self.ring[i], self.rring[i], "d_ring%d" % i

    def consts(self):
        P = self.P
        P.dma("sp", lambda e: e.dma_start(out=self.colp.ap(), in_=self.colp_d), "d_c0", writes=[self.r_colp])
        P.dma("sp", lambda e: e.dma_start(out=self.ident.ap(), in_=self.ident_d), "d_c1", writes=[self.r_ident])
        P.op("dve", lambda e: e.memset(self.ones_bf.ap(), 1.0), writes=[self.r_ones])
        P.dma("pool", lambda e: e.dma_start(out=self.selb.ap(), in_=self.selb_d), "d_c3", writes=[self.r_selb])
        P.op("dve", lambda e: e.tensor_copy(out=self.ident_bf.ap(), in_=self.ident.ap()), reads=[self.r_ident], writes=[self.r_ident])

    def load_x(self, b):
        P = self.P
        for n in range(16):
            xt = self.pgv(9 + n % 2)
            rxt = self.rpg[9 + n % 2]
            P.dma("sp", (lambda xt=xt, n=n: lambda e: e.dma_start(out=xt, in_=self.x[b, n * 128:(n + 1) * 128, :]))(),
                  "d_xt%d" % (n % 2), writes=[rxt])
            for half in range(2):
                bank = (2 * n + half) % 8
                for cc in range(4):
                    c = half * 4 + cc
                    o = self.psv(bank * 512 + cc * 128, bank * 512 + (cc + 1) * 128)
                    i_ = xt[:, c * 128:(c + 1) * 128]
                    P.op("pe", (lambda o=o, i_=i_: lambda e: e.transpose(out=o, in_=i_, identity=self.ident.ap()))(),
                         reads=[rxt, self.r_ident], writes=[self.rps[bank]], milestone=(cc == 3), skip_self=True)
                src = self.psv(bank * 512, (bank + 1) * 512).rearrange("p (c t) -> p c t", c=4)
                dst = cap(self.H.ap(), half * 4 * SEQ + n * 128, [[SEQ, 4], [1, 128]])
                P.op("act", (lambda src=src, dst=dst: lambda e: e.copy(out=dst, in_=src))(),
                     reads=[self.rps[bank]], writes=[self.rH[half * 4 + cc] for cc in range(4)])

    def rmsnorm_fm(self, gname, l):
        P = self.P
        sq = [self.pgv(2, 1, BF16), self.pgv(3, 1, BF16)]
        rsq = [self.rpg[2], self.rpg[3]]
        presummed = getattr(self, "sq_ready", False)
        self.sq_ready = False
        sb0 = 4 if presummed else 0
        for c in range(0 if presummed else 8):
            s, rs = sq[c % 2], rsq[c % 2]
            sqv = s[:, 0:SEQ]
            if c % 2 == 0:
                P.op("act", (lambda sqv=sqv, c=c: lambda e: e.activation(out=sqv, in_=self.hview(c), func=AF.Square))(),
                     reads=[self.rH[c]], writes=[rs])
            else:
                P.op("dve", (lambda sqv=sqv, c=c: lambda e: e.tensor_tensor(out=sqv, in0=self.hview(c), in1=self.hview(c), op=ALU.mult))(),
                     reads=[self.rH[c]], writes=[rs])
            for tt in range(4):
                self.mm(self.psv(tt * 512, (tt + 1) * 512), self.ones_bf.ap(), sqv[:, tt * 512:(tt + 1) * 512],
                        c == 0, c == 7, [rs, self.r_ones], [self.rps[tt]], last=(tt == 3))
        rstd = self.pgv(0, 2)
        r_rstd = self.prs(0, 2)
        P.op("act", lambda e: e.activation(out=rstd, in_=self.psv(sb0 * 512, sb0 * 512 + SEQ), func=AF.Ln, scale=1.0 / D_MODEL, bias=self.eps_ap),
             reads=self.rps[sb0:sb0 + 4] + [self.r_small], writes=r_rstd)
        P.op("act", lambda e: e.activation(out=rstd, in_=rstd, func=AF.Exp, scale=-0.5),
             reads=r_rstd, writes=r_rstd)
        for c in range(8):
            g = self.col(gname, l, c)
            if False:
                tmpn = self.pgv(4, 2)
                P.op("pool", (lambda c=c: lambda e: e.tensor_tensor(out=tmpn, in0=self.hview(c), in1=rstd, op=ALU.mult))(),
                     reads=[self.rH[c]] + r_rstd, writes=self.prs(4, 2))
                P.op("pool", (lambda c=c, g=g: lambda e: e.tensor_scalar(out=self.aview(c), in0=tmpn, scalar1=g, scalar2=None, op0=ALU.mult))(),
                     reads=self.prs(4, 2) + [self.r_colp], writes=[self.rA[c]])
            else:
                P.op("dve", (lambda c=c, g=g: lambda e: e.scalar_tensor_tensor(
                    out=self.aview(c), in0=self.hview(c), scalar=g, in1=rstd, op0=ALU.mult, op1=ALU.mult))(),
                    reads=[self.rH[c], self.r_colp] + r_rstd, writes=[self.rA[c]])

    def ffn(self, l):
        P = self.P
        self.rmsnorm_fm("g_f", l)
        thirds = [(0, 8), (8, 15), (15, 22)]
        tg, tv, gg = self.pgv(4, 2), self.pgv(6, 2), self.pgv(8, 1, BF16)
        r_tg, r_tv, r_gg = self.prs(4, 2), self.prs(6, 2), self.prs(8, 1)
        tvb, r_tvb = self.pgv(9, 1, BF16), self.prs(9)
        for (j0, j1) in thirds:
            for j in range(j0, j1):
                slot, rslot, sk = self.ring_next()
                sv = slot.ap().rearrange("p (k m) -> p k m", k=16)
                srcg = self.w_up[l, :, j * 128:(j + 1) * 128].rearrange("(k p) m -> p k m", p=128)
                srcv = self.w_up[l, :, D_FF + j * 128: D_FF + (j + 1) * 128].rearrange("(k p) m -> p k m", p=128)
                P.dma("pool", (lambda sv=sv, srcg=srcg, srcv=srcv: lambda e: [
                    e.dma_start(out=sv[:, 0:8, :], in_=srcg), e.dma_start(out=sv[:, 8:16, :], in_=srcv)])(),
                    sk, writes=[rslot], n=2)
                for which in range(2):
                    pb = which * 4
                    if j == 0 and which == 0:
                        for k in range(8):
                            for tt in range(4):
                                self.mm(self.psv((pb + tt) * 512, (pb + tt + 1) * 512), sv[:, which * 8 + k, :],
                                        self.aview(k, tt * 512, (tt + 1) * 512), k == 0, k == 7,
                                        [rslot, self.rA[k]], [self.rps[pb + tt]], last=True)
                    else:
                        for tt in range(4):
                            for k in range(8):
                                self.mm(self.psv((pb + tt) * 512, (pb + tt + 1) * 512), sv[:, which * 8 + k, :],
                                        self.aview(k, tt * 512, (tt + 1) * 512), k == 0, k == 7,
                                        [rslot, self.rA[k]], [self.rps[pb + tt]], last=(k == 7))
                    jc = j + which * NFF
                    t = tg if which == 0 else tv
                    rt = r_tg if which == 0 else r_tv
                    pv = self.psv(pb * 512, pb * 512 + SEQ)
                    rp = self.rps[pb:pb + 4]
                    w0, w1, w2, cb = (self.col("cw0", l, jc), self.col("cw1", l, jc), self.col("cw2", l, jc),
                                      self.col("cb", l, jc))
                    P.op("act", (lambda t=t, pv=pv, w1=w1, cb=cb: lambda e: e.activation(
                        out=t, in_=pv, func=AF.Identity, scale=w1, bias=cb))(),
                        reads=rp + [self.r_colp], writes=rt)
                    P.op("dve", (lambda t=t, pv=pv, w0=w0: lambda e: e.scalar_tensor_tensor(
                        out=t[:, 1:SEQ], in0=pv[:, 0:SEQ - 1], scalar=w0, in1=t[:, 1:SEQ],
                        op0=ALU.mult, op1=ALU.add))(), reads=rp + [self.r_colp] + rt, writes=rt)
                    if which == 0:
                        P.op("dve", (lambda t=t, pv=pv, w2=w2: lambda e: e.scalar_tensor_tensor(
                            out=t[:, 0:SEQ - 1], in0=pv[:, 1:SEQ], scalar=w2, in1=t[:, 0:SEQ - 1],
                            op0=ALU.mult, op1=ALU.add))(), reads=rp + [self.r_colp] + rt, writes=rt)
                    else:
                        P.op("dve", (lambda t=t, pv=pv, w2=w2: lambda e: e.scalar_tensor_tensor(
                            out=tvb[:, 0:SEQ - 1], in0=pv[:, 1:SEQ], scalar=w2, in1=t[:, 0:SEQ - 1],
                            op0=ALU.mult, op1=ALU.add))(), reads=rp + [self.r_colp] + rt, writes=r_tvb)
                        P.op("act", (lambda t=t: lambda e: e.copy(out=tvb[:, SEQ - 1:SEQ], in_=t[:, SEQ - 1:SEQ]))(),
                             reads=rt, writes=r_tvb)
                    if which == 0:
                        P.op("act", lambda e: e.activation(out=gg, in_=tg, func=AF.Gelu_apprx_tanh),
                             reads=r_tg, writes=r_gg)
                jj = j - j0
                P.op("dve", (lambda jj=jj: lambda e: e.tensor_tensor(out=self.bview(jj), in0=gg, in1=tvb, op=ALU.mult))(),
                     reads=r_gg + r_tvb, writes=[self.rB[jj]])
            nk = j1 - j0
            for m in range(8):
                slot, rslot, sk = self.ring_next()
                sv = slot.ap()[:, 0:nk * 128].rearrange("p (k m) -> p k m", k=nk)
                src = self.w_down[l, j0 * 128:j1 * 128, m * 128:(m + 1) * 128].rearrange("(k p) m -> p k m", p=128)
                P.dma("pool", (lambda sv=sv, src=src: lambda e: e.dma_start(out=sv, in_=src))(), sk, writes=[rslot])
                last_third = (j1 == NFF)
                want_sq = last_third and bool(self.cfg.mixer) and (l + 1 < self.cfg.depth)
                self.res_chunk(m, (lambda kk, sv=sv: sv[:, kk, :]), [rslot], nk,
                               (lambda kk, tt: self.bview(kk, tt * 512, (tt + 1) * 512)), (lambda kk: [self.rB[kk]]), want_sq)
            if j1 == NFF and bool(self.cfg.mixer) and (l + 1 < self.cfg.depth):
                self.sq_ready = True

    def fm_proj(self, w2d, col0, nmc, rhs_of, rhs_regs, evac, nk=8, kouter_first=False):
        P = self.P
        m = 0
        while m < nmc:
            npair = min(2, nmc - m)
            slot, rslot, sk = self.ring_next()
            sv = slot.ap()[:, 0:nk * npair * 128].rearrange("p (k m) -> p k m", k=nk)
            src = w2d[:, col0 + m * 128: col0 + (m + npair) * 128].rearrange("(k p) m -> p k m", p=128)
            P.dma("pool", (lambda sv=sv, src=src: lambda e: e.dma_start(out=sv, in_=src))(), sk, writes=[rslot])
            for mi in range(npair):
                pb = (self.pb_n % 2) * 4
                self.pb_n += 1
                if kouter_first and m + mi == 0:
                    for k in range(nk):
                        for tt in range(4):
                            self.mm(self.psv((pb + tt) * 512, (pb + tt + 1) * 512), sv[:, k, mi * 128:(mi + 1) * 128],
                                    rhs_of(k, tt), k == 0, k == nk - 1, [rslot] + rhs_regs(k), [self.rps[pb + tt]],
                                    last=True)
                else:
                    for tt in range(4):
                        for k in range(nk):
                            self.mm(self.psv((pb + tt) * 512, (pb + tt + 1) * 512), sv[:, k, mi * 128:(mi + 1) * 128],
                                    rhs_of(k, tt), k == 0, k == nk - 1, [rslot] + rhs_regs(k), [self.rps[pb + tt]],
                                    last=(k == nk - 1))
                evac(m + mi, self.psv(pb * 512, pb * 512 + SEQ), self.rps[pb:pb + 4])
            m += npair

    def fm_proj_res(self, w2d, rhs_of, rhs_regs, nk=8, accumulate_sq=True):
        P = self.P
        m = 0
        while m < 8:
            slot, rslot, sk = self.ring_next()
            sv = slot.ap()[:, 0:nk * 256].rearrange("p (k m) -> p k m", k=nk)
            src = w2d[:, m * 128:(m + 2) * 128].rearrange("(k p) m -> p k m", p=128)
            P.dma("pool", (lambda sv=sv, src=src: lambda e: e.dma_start(out=sv, in_=src))(), sk, writes=[rslot])
            for mi in range(2):
                self.res_chunk(m + mi, lambda k, mi=mi: sv[:, k, mi * 128:(mi + 1) * 128], [rslot], nk, rhs_of, rhs_regs,
                               accumulate_sq)
            m += 2
        if accumulate_sq:
            self.sq_ready = True

    def res_chunk(self, m, lhs_of, lhs_regs, nk, rhs_of, rhs_regs, accumulate_sq):
        P = self.P
        for half in range(2):
            g = self.pb2_n % 2
            self.pb2_n += 1
            for t2 in range(2):
                tt = half * 2 + t2
                bank = 2 * g + t2
                for k in range(nk):
                    self.mm(self.psv(bank * 512, (bank + 1) * 512), lhs_of(k), rhs_of(k, tt), k == 0, k == nk - 1,
                            list(lhs_regs) + rhs_regs(k), [self.rps[bank]], last=(k == nk - 1))
            pv = self.psv(2 * g * 512, (2 * g + 2) * 512)
            hv = self.hview(m, half * 1024, (half + 1) * 1024)
            P.op("dve", (lambda pv=pv, hv=hv: lambda e: e.tensor_tensor(out=hv, in0=pv, in1=hv, op=ALU.add))(),
                 reads=self.rps[2 * g:2 * g + 2] + [self.rH[m]], writes=[self.rH[m]])
        if accumulate_sq:
            sqv = self.pgv(2 + m % 2, 1, BF16)[:, 0:SEQ]
            rs = self.rpg[2 + m % 2]
            P.op("act", (lambda sqv=sqv, m=m: lambda e: e.activation(out=sqv, in_=self.hview(m), func=AF.Square))(),
                 reads=[self.rH[m]], writes=[rs])
            for tt in range(4):
                self.mm(self.psv((4 + tt) * 512, (5 + tt) * 512), self.ones_bf.ap(), sqv[:, tt * 512:(tt + 1) * 512],
                        m == 0, m == 7, [rs, self.r_ones], [self.rps[4 + tt]], last=(tt == 3))

    def resid_add(self, m, pv, prs):
        self.P.op("dve", lambda e: e.tensor_tensor(out=self.hview(m), in0=pv, in1=self.hview(m), op=ALU.add),
                  reads=list(prs) + [self.rH[m]], writes=[self.rH[m]])

    def xattn_kv(self, b, l):
        P = self.P
        sm = self.small.ap()
        memt, r_memt = self.pgv(4), self.prs(4)
        gbc, r_gbc = self.pgv(5), self.prs(5)
        memnT, r_memnT = self.pgv(6, 1, BF16), self.prs(6)
        KT, r_KT = self.pgv(7, 1, BF16), self.prs(7)
        V, r_V = self.pgv(8, 1, BF16), self.prs(8)
        P.dma("sp", lambda e: e.dma_start(out=gbc, in_=self.mem_g[l:l + 1, :].partition_broadcast(128)), "d_gbc", writes=r_gbc)
        for mt in range(2):
            P.dma("sp", (lambda mt=mt: lambda e: e.dma_start(out=memt, in_=self.mem[b, mt * 128:(mt + 1) * 128, :]))(),
                  "d_memt", writes=r_memt)
            junk = self.pgv(9)
            P.op("act", lambda e: e.activation(out=junk, in_=memt, func=AF.Square, accum_out=sm[:, 16:17]),
                 reads=r_memt, writes=self.prs(9) + [self.r_small])
            P.op("act", lambda e: e.activation(out=sm[:, 17:18], in_=sm[:, 16:17], func=AF.Ln, scale=1.0 / D_MODEL, bias=self.eps_ap),
                 reads=[self.r_small], writes=[self.r_small])
            P.op("act", lambda e: e.activation(out=sm[:, 17:18], in_=sm[:, 17:18], func=AF.Exp, scale=-0.5),
                 reads=[self.r_small], writes=[self.r_small])
            P.op("dve", lambda e: e.scalar_tensor_tensor(out=memt, in0=memt, scalar=sm[:, 17:18], in1=gbc, op0=ALU.mult, op1=ALU.mult),
                 reads=r_memt + r_gbc + [self.r_small], writes=r_memt)
            for half in range(2):
                bank = half
                for cc in range(4):
                    c = half * 4 + cc
                    o = self.psv(bank * 512 + cc * 128, bank * 512 + (cc + 1) * 128)
                    i_ = memt[:, c * 128:(c + 1) * 128]
                    P.op("pe", (lambda o=o, i_=i_: lambda e: e.transpose(out=o, in_=i_, identity=self.ident.ap()))(),
                         reads=r_memt + [self.r_ident], writes=[self.rps[bank]], milestone=(cc == 3), skip_self=True)
                src = self.psv(bank * 512, (bank + 1) * 512).rearrange("p (c t) -> p c t", c=4)
                dst = cap(memnT, half * 4 * 256 + mt * 128, [[256, 4], [1, 128]])
                P.op("act", (lambda src=src, dst=dst: lambda e: e.copy(out=dst, in_=src))(),
                     reads=[self.rps[bank]], writes=r_memnT)
        w2d = self.w_kv[l]
        for mp in range(4):
            slot, rslot, sk = self.ring_next()
            sv = slot.ap().rearrange("p (k m) -> p k m", k=8)
            src = w2d[:, mp * 256:(mp + 1) * 256].rearrange("(k p) m -> p k m", p=128)
            P.dma("pool", (lambda sv=sv, src=src: lambda e: e.dma_start(out=sv, in_=src))(), sk, writes=[rslot])
            bank = 2 + (mp % 2)
            for mi in range(2):
                for k in range(8):
                    self.mm(self.psv(bank * 512 + mi * 256, bank * 512 + (mi + 1) * 256), sv[:, k, mi * 128:(mi + 1) * 128],
                            memnT[:, k * 256:(k + 1) * 256], k == 0, k == 7, [rslot] + r_memnT, [self.rps[bank]],
                            last=(k == 7))
            P.op("act", (lambda bank=bank, mp=mp: lambda e: e.copy(out=KT[:, mp * 512:(mp + 1) * 512],
                                                                  in_=self.psv(bank * 512, (bank + 1) * 512)))(),
                 reads=[self.rps[bank]], writes=r_KT)
        for n2 in range(4):
            slot, rslot, sk = self.ring_next()
            sv = slot.ap().rearrange("p (k m) -> p k m", k=8)
            src = w2d[:, D_MODEL + n2 * 256: D_MODEL + (n2 + 1) * 256].rearrange("(k p) m -> p k m", p=128)
            P.dma("pool", (lambda sv=sv, src=src: lambda e: e.dma_start(out=sv, in_=src))(), sk, writes=[rslot])
            bank = 4 + (n2 % 2)
            for mt in range(2):
                for k in range(8):
                    self.mm(self.psv(bank * 512 + mt * 256, bank * 512 + (mt + 1) * 256),
                            memnT[:, k * 256 + mt * 128: k * 256 + (mt + 1) * 128], sv[:, k, :],
                            k == 0, k == 7, [rslot] + r_memnT, [self.rps[bank]], last=(k == 7))
            src_ps = self.psv(bank * 512, (bank + 1) * 512).rearrange("p (t m) -> p t m", t=2)
            dst = cap(V, n2 * 256, [[1024, 2], [1, 256]])
            P.op("act", (lambda src_ps=src_ps, dst=dst: lambda e: e.copy(out=dst, in_=src_ps))(),
                 reads=[self.rps[bank]], writes=r_V)
        self.kv_done = (b, l)

    def xattn(self, b, l):
        P = self.P
        sm = self.small.ap()
        if getattr(self, "kv_done", None) != (b, l):
            self.xattn_kv(b, l)
        KT, r_KT = self.pgv(7, 1, BF16), self.prs(7)
        V, r_V = self.pgv(8, 1, BF16), self.prs(8)
        self.rmsnorm_fm("g_x", l)

        def evac_q(m, pv, prs):
            P.op("act", lambda e: e.copy(out=self.bview(m), in_=pv), reads=list(prs), writes=[self.rB[m]])
        self.fm_proj(self.w_q[l], 0, 8, lambda k, tt: self.aview(k, tt * 512, (tt + 1) * 512),
                     lambda k: [self.rA[k]], evac_q, kouter_first=True)
        def bufs(n):
            par = n % 2
            return dict(par=par, sb0=2 * par, tb=4 + par,
                        Pm=self.pgv(9, 1, BF16)[:, par * 1024:(par + 1) * 1024],
                        PT=self.pgv(10, 1, BF16)[:, par * 1024:(par + 1) * 1024],
                        st=sm[:, 32 + par * 16: 32 + par * 16 + 16])
        if not hasattr(self, "r_st"):
            self.r_st = [Reg("st0"), Reg("st1")]
            self.r_PmPT = [[Reg("Pm0"), Reg("Pm1")], [Reg("PT0"), Reg("PT1")]]
        r_Pm, r_PT = self.r_PmPT
        P.op("dve", lambda e: e.memset(self.pgv(9, 1, BF16)[:, 0:2], 0.0), writes=self.prs(9) + r_Pm)
        P.op("dve", lambda e: e.memset(self.pgv(10, 1, BF16)[:, 0:2], 0.0), writes=self.prs(10) + r_PT)

        def stage_S1(n):
            bf = bufs(n)
            sb0, st = bf["sb0"], bf["st"]
            for h in range(4):
                bank = sb0 + h // 2
                o = self.psv(bank * 512 + (h % 2) * 256, bank * 512 + (h % 2 + 1) * 256)
                for dc in range(2):
                    c = 2 * h + dc
                    self.mm(o, self.bview(c, n * 128, (n + 1) * 128), KT[:, c * 256:(c + 1) * 256], dc == 0, dc == 1,
                            [self.rB[c]] + r_KT, [self.rps[bank]], last=(dc == 1))
            sc = self.psv(sb0 * 512, (sb0 + 2) * 512)
            sc3 = sc.rearrange("p (h m) -> p h m", h=4)
            r_sc = self.rps[sb0:sb0 + 2]
            P.op("dve", lambda e: e.tensor_reduce(out=st[:, 0:4], in_=sc3, axis=AX.X, op=ALU.max),
                 reads=r_sc, writes=[self.r_st[bf["par"]]])
            P.op("dve", lambda e: e.tensor_scalar(out=st[:, 4:8], in0=st[:, 0:4], scalar1=-1.0 / 16.0, scalar2=None, op0=ALU.mult),
                 reads=[self.r_st[bf["par"]]], writes=[self.r_st[bf["par"]]])

        def stage_S2(n):
            bf = bufs(n)
            sb0, Pm, st = bf["sb0"], bf["Pm"], bf["st"]
            rst = self.r_st[bf["par"]]
            sc = self.psv(sb0 * 512, (sb0 + 2) * 512)
            r_sc = self.rps[sb0:sb0 + 2]
            for h in range(4):
                P.op("act", (lambda h=h: lambda e: e.activation(
                    out=Pm[:, h * 256:(h + 1) * 256], in_=sc[:, h * 256:(h + 1) * 256], func=AF.Exp,
                    scale=1.0 / 16.0, bias=st[:, 4 + h:5 + h], accum_out=st[:, 8 + h:9 + h]))(),
                    reads=r_sc + [rst], writes=[r_Pm[bf["par"]], rst])

        def stage_S3(n):
            bf = bufs(n)
            Pm, st = bf["Pm"], bf["st"]
            rst = self.r_st[bf["par"]]
            P.op("dve", lambda e: e.reciprocal(out=st[:, 12:16], in_=st[:, 8:12]), reads=[rst], writes=[rst])
            Pm3 = Pm.rearrange("p (h m) -> p h m", h=4)
            rc3 = cap(st, 12, [[1, 4], [0, 256]])
            P.op("dve", lambda e: e.tensor_tensor(out=Pm3, in0=Pm3, in1=rc3, op=ALU.mult),
                 reads=[r_Pm[bf["par"]], rst], writes=[r_Pm[bf["par"]]])

        def stage_T(n):
            bf = bufs(n)
            tb, Pm, PT = bf["tb"], bf["Pm"], bf["PT"]
            tps = self.psv(tb * 512, (tb + 1) * 512).bitcast(BF16)
            for h in range(4):
                for mh in range(2):
                    j = h * 2 + mh
                    P.op("pe", (lambda j=j, h=h, mh=mh: lambda e: e.transpose(
                        out=tps[:, j * 128:(j + 1) * 128], in_=Pm[:, h * 256 + mh * 128: h * 256 + (mh + 1) * 128],
                        identity=self.ident_bf.ap()))(),
                        reads=[r_Pm[bf["par"]], self.r_ident], writes=[self.rps[tb]], milestone=(j == 7), skip_self=True)
            P.op("act", lambda e: e.copy(out=PT, in_=tps), reads=[self.rps[tb]], writes=[r_PT[bf["par"]]])

        def stage_PV(n):
            bf = bufs(n)
            PT = bf["PT"]
            ob = 6
            for h in range(4):
                for dc in range(2):
                    c = 2 * h + dc
                    bank = ob + c // 4
                    o = self.psv(bank * 512 + (c % 4) * 128, bank * 512 + (c % 4 + 1) * 128)
                    for mh in range(2):
                        self.mm(o, V[:, mh * 1024 + c * 128: mh * 1024 + (c + 1) * 128],
                                PT[:, (h * 2 + mh) * 128:(h * 2 + mh + 1) * 128], mh == 0, mh == 1,
                                r_V + [r_PT[bf["par"]]], [self.rps[bank]], last=(mh == 1))
            for half in range(2):
                bank = ob + half
                src = self.psv(bank * 512, (bank + 1) * 512).rearrange("p (c t) -> p c t", c=4)
                dst = cap(self.A.ap(), half * 4 * SEQ + n * 128, [[SEQ, 4], [1, 128]])
                P.op("dve", (lambda src=src, dst=dst: lambda e: e.tensor_copy(out=dst, in_=src))(),
                     reads=[self.rps[bank]], writes=[self.rA[half * 4 + cc] for cc in range(4)])

        for k in range(-2, 16):
            if 0 <= k + 2 < 16:
                stage_S1(k + 2)
                stage_S2(k + 2)
            if 0 <= k + 1 < 16:
                stage_S3(k + 1)
                stage_T(k + 1)
            if 0 <= k < 16:
                stage_PV(k)
        P.op("dve", lambda e: e.memset(self.pgv(9, 1, BF16)[:, 0:2], 0.0), reads=r_Pm, writes=self.prs(9) + r_Pm)
        P.op("dve", lambda e: e.memset(self.pgv(10, 1, BF16)[:, 0:2], 0.0), reads=r_PT, writes=self.prs(10) + r_PT)
        self.fm_proj_res(self.w_o[l], lambda k, tt: self.aview(k, tt * 512, (tt + 1) * 512),
                         lambda k: [self.rA[k]], accumulate_sq=bool(self.cfg.ffn or (self.cfg.mixer and l + 1 < self.cfg.depth)))

    def mixer(self, b, l):
        P = self.P
        cfg = self.cfg
        sm = self.small.ap()
        self.rmsnorm_fm("g_mix", l)
        gvbc = self.pgv(4)[:, 0:512]
        bsbc = self.pgv(4)[:, 512:1024]
        wsT = self.pgv(5, 1, BF16)[:, 0:512]
        P.dma("sp", lambda e: [e.dma_start(out=gvbc, in_=self.gv[l:l + 1, :].partition_broadcast(128)),
                               e.dma_start(out=bsbc, in_=self.bs[l:l + 1, :].partition_broadcast(128))],
              "d_mx0", writes=self.prs(4), n=2)
        P.dma("pool", lambda e: e.dma_start(out=wsT, in_=self.wsT[l]), "d_mx1", writes=self.prs(5))
        zr = lambda k, tt: self.aview(k, tt * 512, (tt + 1) * 512)
        zreg = lambda k: [self.rA[k]]

        def evac_ug(m, pv, prs):
            P.op("act", lambda e: e.activation(out=self.bview(4 + m), in_=pv, func=AF.Gelu_apprx_tanh),
                 reads=list(prs), writes=[self.rB[4 + m]])

        def evac_us(m, pv, prs):
            P.op("act", lambda e: e.copy(out=self.bview(m), in_=pv), reads=list(prs), writes=[self.rB[m]])
        self.fm_proj(self.w_in[l], 512, 4, zr, zreg, evac_ug, kouter_first=True)
        self.fm_proj(self.w_in[l], 0, 4, zr, zreg, evac_us)
        pans = []
        for half in range(2):
            slot, rslot, sk = self.ring_next()
            sv = slot.ap().rearrange("p (k m) -> p k m", k=8)
            src = self.w_in[l][:, 1024 + half * 256: 1024 + (half + 1) * 256].rearrange("(k p) m -> p k m", p=128)
            P.dma("pool", (lambda sv=sv, src=src: lambda e: e.dma_start(out=sv, in_=src))(), sk, writes=[rslot])
            pans.append((sv, rslot))
        if not hasattr(self, "r_vp"):
            self.r_vp = {nm: [Reg(nm + "0"), Reg(nm + "1")] for nm in ("vt", "vn", "tmp", "st")}
        rv = self.r_vp
        guard_pages = self.prs(6) + self.prs(7) + self.prs(8)
        allv = [r for nm in ("vt", "vn", "tmp") for r in rv[nm]]
        P.op("dve", lambda e: e.memset(self.pgv(6)[:, 0:1], 0.0), writes=guard_pages + allv)

        def vbufs(n):
            par = n % 2
            return dict(par=par, vb=par, sbk=2 + par,
                        vt=self.pgv(6)[:, par * 512:(par + 1) * 512],
                        vn=self.pgv(7, 1, BF16)[:, par * 512:(par + 1) * 512],
                        tmp=self.pgv(8)[:, par * 512:(par + 1) * 512],
                        st=sm[:, 64 + par * 4: 64 + par * 4 + 4])

        def V12(n):
            bf = vbufs(n)
            par, vb, vt, tmp, st = bf["par"], bf["vb"], bf["vt"], bf["tmp"], bf["st"]
            for half in range(2):
                sv, rslot = pans[half]
                for k in range(8):
                    self.mm(self.psv(vb * 512 + half * 256, vb * 512 + (half + 1) * 256),
                            self.aview(k, n * 128, (n + 1) * 128), sv[:, k, :], k == 0, k == 7,
                            [rslot, self.rA[k]], [self.rps[vb]], last=(k == 7))
            vps = self.psv(vb * 512, (vb + 1) * 512)
            P.op("act", lambda e: e.activation(out=vt, in_=vps, func=AF.Gelu_apprx_tanh),
                 reads=[self.rps[vb]], writes=[rv["vt"][par]])
            P.op("dve", lambda e: e.scalar_tensor_tensor(out=tmp, in0=vt, scalar=1.0, in1=vt, op0=ALU.mult, op1=ALU.mult,
                                                         accum_out=st[:, 0:1]),
                 reads=[rv["vt"][par]], writes=[rv["tmp"][par], rv["st"][par]])
            P.op("act", lambda e: e.activation(out=st[:, 1:2], in_=st[:, 0:1], func=AF.Ln, scale=1.0 / 512.0, bias=self.eps_ap),
                 reads=[rv["st"][par], self.r_small], writes=[rv["st"][par]])
            P.op("act", lambda e: e.activation(out=st[:, 1:2], in_=st[:, 1:2], func=AF.Exp, scale=-0.5),
                 reads=[rv["st"][par]], writes=[rv["st"][par]])

        def V34(n):
            bf = vbufs(n)
            par, sbk, vt, vn, st = bf["par"], bf["sbk"], bf["vt"], bf["vn"], bf["st"]
            P.op("dve", lambda e: e.scalar_tensor_tensor(out=vn, in0=vt, scalar=st[:, 1:2], in1=gvbc, op0=ALU.mult, op1=ALU.mult),
                 reads=[rv["vt"][par], rv["st"][par]] + self.prs(4), writes=[rv["vn"][par]])
            for h in range(4):
                self.mm(self.psv(sbk * 512 + h * 128, sbk * 512 + (h + 1) * 128), vn[:, h * 128:(h + 1) * 128],
                        wsT[:, h * 128:(h + 1) * 128], True, True, [rv["vn"][par]] + self.prs(5), [self.rps[sbk]], last=(h == 3))

        def V5(n):
            bf = vbufs(n)
            par, sbk, tmp = bf["par"], bf["sbk"], bf["tmp"]
            P.op("dve", lambda e: e.tensor_tensor(out=tmp, in0=self.psv(sbk * 512, (sbk + 1) * 512), in1=bsbc, op=ALU.add),
                 reads=[self.rps[sbk]] + self.prs(4), writes=[rv["tmp"][par]])
            ugv = cap(self.B.ap(), 4 * SEQ + n * 128, [[SEQ, 4], [1, 128]])
            tmp3 = tmp.rearrange("p (h q) -> p h q", h=4)
            P.op("dve", lambda e: e.tensor_tensor(out=ugv, in0=tmp3, in1=ugv, op=ALU.mult),
                 reads=[rv["tmp"][par]] + self.rB[4:8], writes=self.rB[4:8])

        for k in range(-2, 16):
            if 0 <= k + 2 < 16:
                V12(k + 2)
            if 0 <= k + 1 < 16:
                V34(k + 1)
            if 0 <= k < 16:
                V5(k)
        P.op("dve", lambda e: e.memset(self.pgv(6)[:, 0:1], 0.0), reads=allv, writes=guard_pages + allv)
        if cfg.mixer == "g":
            for m in range(4):
                P.op("dve", (lambda m=m: lambda e: e.memset(self.aview(m), 0.0))(), writes=[self.rA[m]])
        else:
            self.s5(b, l)
        yr = lambda k, tt: (self.aview(k, tt * 512, (tt + 1) * 512) if k < 4 else self.bview(k, tt * 512, (tt + 1) * 512))
        yreg = lambda k: [self.rA[k]] if k < 4 else [self.rB[k]]
        if cfg.xattn:
            self.xattn_kv(b, l)
        self.fm_proj_res(self.w_out[l], yr, yreg, accumulate_sq=bool(cfg.xattn or cfg.ffn))

    def range_reduce(self, X, rX, TF, rTF, pre_add=0.0):
        P = self.P
        PI = math.pi
        C1 = 6.28125
        C2 = 2 * PI - C1
        TI = TF.bitcast(mybir.dt.int32)
        if pre_add != 0.0:
            P.op("dve", lambda e: e.tensor_scalar(out=X, in0=X, scalar1=pre_add, scalar2=None, op0=ALU.add), reads=rX, writes=rX)
        P.op("dve", lambda e: e.tensor_scalar(out=TF, in0=X, scalar1=1.0 / (2 * PI), scalar2=None, op0=ALU.mult), reads=rX, writes=rTF)
        P.op("dve", lambda e: e.tensor_copy(out=TI, in_=TF), reads=rTF, writes=rTF)
        P.op("dve", lambda e: e.tensor_copy(out=TF, in_=TI), reads=rTF, writes=rTF)
        P.op("dve", lambda e: e.scalar_tensor_tensor(out=X, in0=TF, scalar=-C1, in1=X, op0=ALU.mult, op1=ALU.add), reads=rTF + rX, writes=rX)
        P.op("dve", lambda e: e.scalar_tensor_tensor(out=X, in0=TF, scalar=-C2, in1=X, op0=ALU.mult, op1=ALU.add), reads=rTF + rX, writes=rX)
        P.op("dve", lambda e: e.tensor_scalar(out=TF, in0=X, scalar1=PI, scalar2=None, op0=ALU.is_gt), reads=rX, writes=rTF)
        P.op("dve", lambda e: e.scalar_tensor_tensor(out=X, in0=TF, scalar=-2 * PI, in1=X, op0=ALU.mult, op1=ALU.add), reads=rTF + rX, writes=rX)
        P.op("dve", lambda e: e.tensor_scalar(out=TF, in0=X, scalar1=-PI, scalar2=None, op0=ALU.is_lt), reads=rX, writes=rTF)
        P.op("dve", lambda e: e.scalar_tensor_tensor(out=X, in0=TF, scalar=2 * PI, in1=X, op0=ALU.mult, op1=ALU.add), reads=rTF + rX, writes=rX)
        P.op("dve", lambda e: e.tensor_scalar(out=X, in0=X, scalar1=-PI, scalar2=PI, op0=ALU.max, op1=ALU.min), reads=rX, writes=rX)

    def s5_prologue(self, l):
        P = self.P
        TT = lambda out, in0, in1, op, reads, writes, eng="dve": P.op(
            eng, lambda e: e.tensor_tensor(out=out, in0=in0, in1=in1, op=op), reads=reads, writes=writes)
        pA, rA_ = self.pgv(0), self.prs(0)
        pBm, rBm = self.pgv(1), self.prs(1)
        pCm, rCm = self.pgv(2), self.prs(2)
        pK, rK = self.pgv(13), self.prs(13)
        MAGK, rMAG = self.pgv(3), self.prs(3)
        Li, rLi = self.pgv(4), self.prs(4)
        Lr, rLr = self.pgv(5), self.prs(5)
        Bb, rBb = self.pgv(6), self.prs(6)
        self.r_s5w = getattr(self, "r_s5w", {})
        self.r_tab = getattr(self, "r_tab", {})
        PI = math.pi
        P.dma("sp", lambda e: [e.dma_start(out=pA[:, 0:96], in_=self.s5p[l]), e.dma_start(out=pBm, in_=self.s5b[l]),
                               e.dma_start(out=pCm, in_=self.s5c[l]), e.dma_start(out=pK, in_=self.s5k)],
              "d_s5in", writes=rA_ + rBm + rCm + rK, n=4)
        ar, ai, ldt = pA[:, 0:32], pA[:, 32:64], pA[:, 64:96]
        dt, adt, ang, den = pA[:, 96:128], pA[:, 128:160], pA[:, 160:192], pA[:, 192:224]
        fre, fim, ta, tb, ang8 = pA[:, 224:256], pA[:, 256:288], pA[:, 288:320], pA[:, 320:352], pA[:, 352:384]
        tc_, rden = pA[:, 384:416], pA[:, 416:448]
        KV = pK[:, 0:26]
        sgn = pK[:, 26:27]
        cidx = pK[:, 32:288]
        jmask = pK[:, 288:544]
        maskF = pK[:, 544:672]
        maskB = pK[:, 672:800]
        rw = rA_
        P.op("act", lambda e: e.activation(out=dt, in_=ldt, func=AF.Exp), reads=rw, writes=rw)
        TT(adt, ar, dt, ALU.mult, rw, rw)
        TT(ang, ai, dt, ALU.mult, rw, rw)
        b3 = lambda v: cap(v, 0, [[1, 32], [0, 26]])
        k3 = cap(KV, 0, [[0, 32], [1, 26]])
        M3 = MAGK[:, 0:832].rearrange("p (g k) -> p g k", g=32)
        Li3 = Li[:, 0:832].rearrange("p (g k) -> p g k", g=32)
        Lr3 = Lr[:, 0:832].rearrange("p (g k) -> p g k", g=32)
        TT(M3, b3(adt), k3, ALU.mult, rw + rK, rMAG)
        P.op("act", lambda e: e.activation(out=MAGK[:, 0:832], in_=MAGK[:, 0:832], func=AF.Exp), reads=rMAG, writes=rMAG)
        TT(Li3, b3(ang), k3, ALU.mult, rw + rK, rLi)
        TF7, rTF7 = self.pgv(7), self.prs(7)
        P.op("dve", lambda e: e.tensor_copy(out=Lr[:, 0:832], in_=Li[:, 0:832]), reads=rLi, writes=rLr)
        self.range_reduce(Lr[:, 0:832], rLr, TF7[:, 0:832], rTF7, pre_add=PI / 2)
        self.range_reduce(Li[:, 0:832], rLi, TF7[:, 0:832], rTF7)
        for (T_, rT) in ((Lr, rLr), (Li, rLi)):
            P.op("act", (lambda T_=T_: lambda e: e.activation(out=T_[:, 0:832], in_=T_[:, 0:832], func=AF.Sin))(), reads=rT, writes=rT)
            TT(T_[:, 0:832], T_[:, 0:832], MAGK[:, 0:832], ALU.mult, rT + rMAG, rT)
        kcol = lambda T_, k: cap(T_, k, [[26, 32]])
        Lr1, Li1 = kcol(Lr, 25), kcol(Li, 25)
        TT(den, ar, ar, ALU.mult, rw, rw)
        TT(ta, ai, ai, ALU.mult, rw, rw)
        TT(den, den, ta, ALU.add, rw, rw)
        P.op("dve", lambda e: e.reciprocal(out=rden, in_=den), reads=rw, writes=rw)
        P.op("dve", lambda e: e.tensor_scalar(out=tc_, in0=Lr1, scalar1=-1.0, scalar2=None, op0=ALU.add), reads=rLr, writes=rw)
        TT(ta, tc_, ar, ALU.mult, rw, rw)
        TT(tb, Li1, ai, ALU.mult, rw + rLi, rw)
        TT(fre, ta, tb, ALU.add, rw, rw)
        TT(fre, fre, rden, ALU.mult, rw, rw)
        TT(ta, Li1, ar, ALU.mult, rw + rLi, rw)
        TT(tb, tc_, ai, ALU.mult, rw, rw)
        TT(fim, ta, tb, ALU.subtract, rw, rw)
        TT(fim, fim, rden, ALU.mult, rw, rw)
        P.op("dve", lambda e: e.tensor_scalar(out=ang8, in0=ang, scalar1=8.0, scalar2=None, op0=ALU.mult), reads=rw, writes=rw)
        T1, rT1 = self.pgv(7), self.prs(7)
        T2, rT2 = self.pgv(8), self.prs(8)
        g16 = lambda v: cap(v, 0, [[1, 32], [0, 16]])
        v3 = lambda v: v.rearrange("p (g h) -> p g h", g=32)
        Bre, Bim = pBm[:, 0:512], pBm[:, 512:1024]
        Bbr, Bbi = Bb[:, 0:512], Bb[:, 512:1024]
        TT(v3(T1[:, 0:512]), v3(Bre), g16(fre), ALU.mult, rBm + rw, rT1)
        TT(v3(T1[:, 512:1024]), v3(Bim), g16(fim), ALU.mult, rBm + rw, rT1)
        TT(Bbr, T1[:, 0:512], T1[:, 512:1024], ALU.subtract, rT1, rBb)
        TT(v3(T1[:, 0:512]), v3(Bim), g16(fre), ALU.mult, rBm + rw, rT1)
        TT(v3(T1[:, 512:1024]), v3(Bre), g16(fim), ALU.mult, rBm + rw, rT1)
        TT(Bbi, T1[:, 0:512], T1[:, 512:1024], ALU.add, rT1, rBb)
        Cre, Cim = pCm[:, 0:512], pCm[:, 512:1024]
        stg_n = [0]

        def stage_out(src, rsrc, dram_ap, rdram, eng="act"):
            i = stg_n[0] % 2
            stg_n[0] += 1
            stg = self.pgv(12, 1, BF16)[:, i * 1024:(i + 1) * 1024]
            rstg = self.prs(12)
            if eng == "act":
                P.op("act", lambda e: e.copy(out=stg, in_=src), reads=rsrc, writes=rstg)
            else:
                P.op("dve", lambda e: e.tensor_copy(out=stg, in_=src), reads=rsrc, writes=rstg)
            P.dma("sp", lambda e: e.dma_start(out=dram_ap, in_=stg), "d_stg%d" % i, reads=rstg, writes=[rdram])

        for q in range(4):
            regs = [Reg("s5w_%d_%d_%d" % (l, q, i)) for i in range(5)]
            self.r_s5w[(l, q)] = regs
            def Lv(T_, k0):
                return cap(T_, q * 8 * 26 + k0, [[26, 8], [1, 8], [0, 16]])

            def Xv(T_):
                return cap(T_, q * 128, [[16, 8], [0, 8], [1, 16]])
            o4 = lambda T_: T_.rearrange("p (g s h) -> p g s h", g=8, s=8)
            WA, rWA = self.pgv(9), self.prs(9)
            WB, rWB = self.pgv(10), self.prs(10)
            WY, rWY = self.pgv(11), self.prs(11)
            for (k0, sec, keep) in ((0, 0, False), (8, None, True)):
                TT(o4(WA), Lv(Lr, k0), Xv(Bbr), ALU.mult, rLr + rBb, rWA)
                TT(o4(T1), Lv(Li, k0), Xv(Bbi), ALU.mult, rLi + rBb, rT1)
                TT(WA, WA, T1, ALU.subtract, rWA + rT1, rWA)
                TP, rTP = self.pgv(12), self.prs(12)
                TT(o4(WB), Lv(Lr, k0), Xv(Bbi), ALU.mult, rLr + rBb, rWB, "dve")
                TT(o4(TP), Lv(Li, k0), Xv(Bbr), ALU.mult, rLi + rBb, rTP, "dve")
                TT(WB, WB, TP, ALU.add, rWB + rTP, rWB, "dve")
                if not keep:
                    for wi, (W_, rW_) in enumerate(((WA, rWA), (WB, rWB))):
                        for hb in range(2):
                            bank = hb
                            for gi in range(4):
                                g = hb * 4 + gi
                                o = self.psv(bank * 512 + gi * 128, bank * 512 + (gi + 1) * 128)
                                i_ = W_[:, g * 128:(g + 1) * 128]
                                P.op("pe", (lambda o=o, i_=i_: lambda e: e.transpose(out=o, in_=i_, identity=self.ident.ap()))(),
                                     reads=rW_ + [self.r_ident], writes=[self.rps[bank]], milestone=(gi == 3), skip_self=True)
                            P.op("act", (lambda bank=bank, hb=hb: lambda e: e.copy(
                                out=T2.bitcast(BF16)[:, hb * 512:(hb + 1) * 512], in_=self.psv(bank * 512, (bank + 1) * 512)))(),
                                reads=[self.rps[bank]], writes=rT2)
                        P.dma("sp", (lambda wi=wi, q=q: lambda e: e.dma_start(
                            out=self.s5w_d[l, q, :, wi * 1024:(wi + 1) * 1024], in_=T2.bitcast(BF16)[:, 0:1024]))(),
                            "d_stgT", reads=rT2, writes=[regs[wi]])
            def Cv(T_):
                return cap(T_, q * 128, [[16, 8], [0, 8], [1, 16]])

            def Ly(T_):
                return cap(T_, q * 8 * 26 + 16, [[26, 8], [1, 8], [0, 16]])
            WYi, rWYi = self.pgv(3), self.prs(3)
            WYi, rWYi = T2, rT2
            TT(o4(WY), Cv(Cre), Ly(Lr), ALU.mult, rCm + rLr, rWY)
            TT(o4(T1), Cv(Cim), Ly(Li), ALU.mult, rCm + rLi, rT1)
            TT(WY, WY, T1, ALU.subtract, rWY + rT1, rWY)
            TP, rTP = self.pgv(12), self.prs(12)
            TT(o4(WYi), Cv(Cre), Ly(Li), ALU.mult, rCm + rLi, rWYi, "dve")
            TT(o4(TP), Cv(Cim), Ly(Lr), ALU.mult, rCm + rLr, rTP, "dve")
            TT(WYi, WYi, TP, ALU.add, rWYi + rTP, rWYi, "dve")
            P.op("act", lambda e: e.mul(out=WYi, in_=WYi, mul=-1.0), reads=rWYi, writes=rWYi)
            stage_out(WY, rWY, self.s5w_d[l, q, :, 2048:3072], regs[2])
            stage_out(WYi, rWYi, self.s5w_d[l, q, :, 3072:4096], regs[3])
            mY, rmY = self.pgv(12), self.prs(12)
            mYi, rmYi = T1, rT1
            for d_ in range(2):
                rm = pK[:, 800 + d_: 801 + d_]
                P.op("act", (lambda rm=rm: lambda e: e.mul(out=mY, in_=WY, mul=rm))(),
                     reads=rWY + rK, writes=rmY)
                P.op("act", (lambda rm=rm: lambda e: e.mul(out=mYi, in_=WYi, mul=rm))(),
                     reads=rWYi + rK, writes=rmYi)
                for g in range(8):
                    bank = 2 + 2 * d_ + g // 4
                    o = self.psv(bank * 512 + (g % 4) * 128, bank * 512 + (g % 4 + 1) * 128)
                    self.mm(o, WA[:, g * 128:(g + 1) * 128], mY[:, g * 128:(g + 1) * 128], True, False,
                            rWA + rmY, [self.rps[bank]], last=False)
                    self.mm(o, WB[:, g * 128:(g + 1) * 128], mYi[:, g * 128:(g + 1) * 128], False, True,
                            rWB + rmYi, [self.rps[bank]], last=True)
            mF = cap(maskF, 0, [[0, 4], [1, 128]])
            mB = cap(maskB, 0, [[0, 4], [1, 128]])
            for hb in range(2):
                Mh3 = T1[:, hb * 512:(hb + 1) * 512].rearrange("p (g m) -> p g m", g=4)
                pf = self.psv((2 + hb) * 512, (3 + hb) * 512).rearrange("p (g m) -> p g m", g=4)
                pb_ = self.psv((4 + hb) * 512, (5 + hb) * 512).rearrange("p (g m) -> p g m", g=4)
                TT(Mh3, pf, mF, ALU.mult, [self.rps[2 + hb]] + rK, rT1)
                P.op("dve", (lambda pb_=pb_: lambda e: e.tensor_tensor(out=pb_, in0=pb_, in1=mB, op=ALU.mult))(),
                     reads=[self.rps[4 + hb]] + rK, writes=[self.rps[4 + hb]])
                TT(Mh3, Mh3, pb_, ALU.add, rT1 + [self.rps[4 + hb]], rT1)
            stage_out(T1, rT1, self.s5w_d[l, q, :, 4096:5120], regs[4], eng="dve")
        kidx = pK[:, 832:864]
        for q in range(4):
            CS, rCS = self.pgv(9, 2), self.prs(9, 2)
            SN, rSN = self.pgv(7, 2), self.prs(7, 2)
            RH, rRH = self.pgv(11, 2), self.prs(11, 2)
            SM, rSM = self.pgv(6), self.prs(6)
            sC, sS, sT = SM[:, 0:256], SM[:, 256:512], SM[:, 512:768]
            s3 = lambda T_: T_.rearrange("p (g k) -> p g k", g=8)
            a8 = cap(ang8, q * 8, [[1, 8], [0, 32]])
            kk = cap(kidx, 0, [[0, 8], [1, 32]])
            TT(s3(sS), a8, kk, ALU.mult, rw + rK, rSM)
            P.op("dve", lambda e: e.tensor_copy(out=sC, in_=sS), reads=rSM, writes=rSM)
            self.range_reduce(sC, rSM, sT, rSM, pre_add=PI / 2)
            self.range_reduce(sS, rSM, sT, rSM)
            P.op("act", lambda e: e.activation(out=SM[:, 0:512], in_=SM[:, 0:512], func=AF.Sin), reads=rSM, writes=rSM)
            P.op("dve", lambda e: e.tensor_scalar(out=sS, in0=sS, scalar1=sgn, scalar2=None, op0=ALU.mult), reads=rSM + rK, writes=rSM)
            vA = lambda T_: cap(T_, 0, [[32, 8], [1, 16], [0, 16]])
            vB = lambda T_: cap(T_, 16, [[32, 8], [0, 16], [1, 16]])
            o4 = lambda T_: T_.rearrange("p (g a b) -> p g a b", g=8, a=16)
            TT(o4(CS), vA(sC), vB(sC), ALU.mult, rSM, rCS)
            TT(o4(RH), vA(sS), vB(sS), ALU.mult, rSM, rRH)
            TT(CS, CS, RH, ALU.subtract, rCS + rRH, rCS)
            TQ, rTQ = self.pgv(1, 2), self.prs(1, 2)
            TT(o4(SN), vA(sS), vB(sC), ALU.mult, rSM, rSN, "dve")
            TT(o4(TQ), vA(sC), vB(sS), ALU.mult, rSM, rTQ, "dve")
            TT(SN, SN, TQ, ALU.add, rSN + rTQ, rSN, "dve")
            t3 = lambda T_: T_.rearrange("p (g j) -> p g j", g=8)
            m8 = cap(MAGK, q * 8 * 26 + 24, [[26, 8], [0, 256]])
            jm = cap(jmask, 0, [[0, 8], [1, 256]])
            TT(t3(RH), m8, jm, ALU.mult, rMAG + rK, rRH)
            CSb, rCSb = self.pgv(4, 1, BF16), self.prs(4)
            SNb, rSNb = self.pgv(5, 1, BF16), self.prs(5)
            P.op("act", lambda e: e.copy(out=CSb, in_=CS), reads=rCS, writes=rCSb)
            P.op("act", lambda e: e.copy(out=SNb, in_=SN), reads=rSN, writes=rSNb)
            for hb in range(2):
                rg = Reg("tab_%d_%d" % (l, q * 2 + hb))
                self.r_tab[(l, q * 2 + hb)] = rg
                dstb = self.tabb_d[l, q * 2 + hb]
                dstr = self.rho_d[l, q * 2 + hb]
                P.dma("sp", (lambda dstb=dstb, dstr=dstr, hb=hb: lambda e: [
                    e.dma_start(out=dstb[:, 0:1024], in_=CSb[:, hb * 1024:(hb + 1) * 1024]),
                    e.dma_start(out=dstb[:, 1024:2048], in_=SNb[:, hb * 1024:(hb + 1) * 1024]),
                    e.dma_start(out=dstr, in_=RH[:, hb * 1024:(hb + 1) * 1024])])(),
                    "d_tabw", reads=rCSb + rSNb + rRH, writes=[rg], n=3)

    def s5(self, b, l):
        P = self.P
        TT = lambda out, in0, in1, op, reads, writes: P.op(
            "dve", lambda e: e.tensor_tensor(out=out, in0=in0, in1=in1, op=op), reads=reads, writes=writes)
        LZ, rLZ = self.pgv(0, 1, BF16), self.prs(0)
        WYM, rWYM = self.pgv(1, 2, BF16), self.prs(1, 2)
        TABS = [self.pgv(3, 3), self.A.ap()[:, 4 * SEQ:7 * SEQ].bitcast(F32)]
        rTABS = [self.prs(3, 3), self.rA[4:7]]
        UQ = [self.pgv(6, 1, BF16), self.aview(7)]
        rUQ = [self.prs(6), [self.rA[7]]]
        Yq, rYq = self.pgv(7, 1, BF16), self.prs(7)
        T1, rT1 = self.pgv(8), self.prs(8)
        T2, rT2 = self.pgv(9), self.prs(9)
        Av, rAv = self.pgv(10), self.prs(10)
        Bv, rBv = self.pgv(11), self.prs(11)
        XSS = [self.pgv(12, 1, BF16), self.pgv(13, 1, BF16)]
        rXSS = [self.prs(12), self.prs(13)]
        selb = self.selb.ap()
        ZRE, ZIM = self.psv(2 * 512, 4 * 512), self.psv(4 * 512, 6 * 512)
        rZRE, rZIM = self.rps[2:4], self.rps[4:6]
        LZre = lambda g: LZ[:, g * 128:(g + 1) * 128]
        LZim = lambda g: LZ[:, 1024 + g * 128: 1024 + (g + 1) * 128]
        WYr = lambda g: WYM[:, g * 128:(g + 1) * 128]
        WYi = lambda g: WYM[:, 1024 + g * 128: 1024 + (g + 1) * 128]
        Mg = lambda g: WYM[:, 2048 + g * 128: 2048 + (g + 1) * 128]

        def load_LZ(q):
            P.dma("sp", lambda e: e.dma_start(out=LZ, in_=self.s5w_d[l, q, :, 0:2048]), "d_w5a",
                  reads=self.r_s5w[(l, q)][0:2], writes=rLZ)

        def load_WYM(q):
            P.dma("sp", lambda e: e.dma_start(out=WYM[:, 0:3072], in_=self.s5w_d[l, q, :, 2048:5120]), "d_w5b",
                  reads=self.r_s5w[(l, q)][2:5], writes=rWYM)

        def load_TAB(q, half):
            hb = 2 * q + half
            tb16 = TABS[half].bitcast(BF16)
            P.dma("sp", lambda e: [e.dma_start(out=tb16[:, 0:2048], in_=self.tabb_d[l, hb]),
                                   e.dma_start(out=TABS[half][:, 1024:2048], in_=self.rho_d[l, hb])], "d_tab%d" % half,
                  reads=[self.r_tab[(l, hb)]], writes=rTABS[half], n=2)

        def SEL(q):
            Uq, rUq = UQ[q % 2], rUQ[q % 2]
            for g8 in range(8):
                bank = 6 + (g8 // 2) % 2
                o = self.psv(bank * 512 + (g8 % 2) * 256, bank * 512 + (g8 % 2 + 1) * 256)
                for s0 in range(8):
                    self.mm(o, selb[:, g8 * 240 + (7 - s0) * 16: g8 * 240 + (7 - s0) * 16 + 128],
                            self.bview(q)[:, s0:SEQ:8], s0 == 0, s0 == 7, [self.r_selb, self.rB[q]], [self.rps[bank]],
                            last=(s0 == 7))
                if g8 % 2 == 1:
                    P.op("act", (lambda bank=bank, g8=g8: lambda e: e.copy(
                        out=Uq[:, (g8 - 1) * 256:(g8 + 1) * 256], in_=self.psv(bank * 512, (bank + 1) * 512)))(),
                        reads=[self.rps[bank]], writes=rUq)

        def Zmm(q, half):
            Uq, rUq = UQ[q % 2], rUQ[q % 2]
            for gi in range(4):
                g8 = half * 4 + gi
                U = Uq[:, g8 * 256:(g8 + 1) * 256]
                for (Z, rZ, LZf) in ((ZRE, rZRE, LZre), (ZIM, rZIM, LZim)):
                    self.mm(Z[:, gi * 256:(gi + 1) * 256], LZf(g8), U, True, True, rLZ + rUq, [rZ[gi // 2]], last=True)

        def Bst(q, half):
            TAB, rTAB = TABS[half], rTABS[half]
            XS, rXS = XSS[half], rXSS[half]
            tb16 = TAB.bitcast(BF16)
            COS, SIN, RHO = tb16[:, 0:1024], tb16[:, 1024:2048], TAB[:, 1024:2048]
            b16 = lambda T_: T_.bitcast(BF16)[:, 0:1024]
            T1b, T2b, Avb, Bvb = b16(T1), b16(T2), b16(Av), b16(Bv)
            TT(T1b, ZRE, COS, ALU.mult, rZRE + rTAB, rT1)
            TT(T2b, ZIM, SIN, ALU.mult, rZIM + rTAB, rT2)
            TT(Avb, T1b, T2b, ALU.add, rT1 + rT2, rAv)
            TT(T1b, ZIM, COS, ALU.mult, rZIM + rTAB, rT1)
            TT(T2b, ZRE, SIN, ALU.mult, rZRE + rTAB, rT2)
            TT(Bvb, T1b, T2b, ALU.subtract, rT1 + rT2, rBv)
            Wr, Wi, rWr, rWi = T1b, T2b, rT1, rT2
            f_ = lambda T_: T_[0:64, :]
            r_ = lambda T_: cap(T_[64:128, :], 1023, [[-1, 1024]])
            for (W_, rW_, S_, rS_) in ((Wr, rWr, Avb, rAv), (Wi, rWi, Bvb, rBv)):
                P.op("dve", (lambda W_=W_, S_=S_: lambda e: e.tensor_tensor_scan(
                    out=f_(W_), data0=f_(RHO), data1=f_(S_), initial=0.0, op0=ALU.mult, op1=ALU.add))(),
                    reads=rTAB + rS_, writes=rW_)
                P.op("dve", (lambda W_=W_, S_=S_: lambda e: e.tensor_tensor_scan(
                    out=r_(W_), data0=r_(RHO), data1=r_(S_), initial=0.0, op0=ALU.mult, op1=ALU.add))(),
                    reads=rTAB + rS_, writes=rW_)
            P1, P2, rP1, rP2 = Avb, Bvb, rAv, rBv
            fo = lambda off: cap(XS[0:64, :], off + 1, [[256, 4], [1, 255]])
            fi = lambda T_: cap(T_[0:64, :], 0, [[256, 4], [1, 255]])
            bo = lambda off: cap(XS[64:128, :], off, [[256, 4], [1, 255]])
            bi = lambda T_: cap(T_[64:128, :], 1, [[256, 4], [1, 255]])
            TT(P1, Wr, COS, ALU.mult, rWr + rTAB, rP1)
            TT(P2, Wi, SIN, ALU.mult, rWi + rTAB, rP2)
            P.op("dve", lambda e: e.memset(cap(XS[0:64, :], 0, [[256, 8], [1, 1]]), 0.0), writes=rXS)
            P.op("dve", lambda e: e.memset(cap(XS[64:128, :], 255, [[256, 8], [1, 1]]), 0.0), writes=rXS)
            TT(fo(0), fi(P1), fi(P2), ALU.subtract, rP1 + rP2, rXS)
            TT(bo(0), bi(P1), bi(P2), ALU.subtract, rP1 + rP2, rXS)
            TT(P1, Wr, SIN, ALU.mult, rWr + rTAB, rP1)
            TT(P2, Wi, COS, ALU.mult, rWi + rTAB, rP2)
            TT(fo(1024), fi(P1), fi(P2), ALU.add, rP1 + rP2, rXS)
            TT(bo(1024), bi(P1), bi(P2), ALU.add, rP1 + rP2, rXS)

        def Ymm(q, half):
            Uq, rUq = UQ[q % 2], rUQ[q % 2]
            XS, rXS = XSS[half], rXSS[half]
            for gi in range(4):
                g8 = half * 4 + gi
                bank = 6 + gi // 2
                o = self.psv(bank * 512 + (gi % 2) * 256, bank * 512 + (gi % 2 + 1) * 256)
                U = Uq[:, g8 * 256:(g8 + 1) * 256]
                rr = rWYM + rUq + rXS
                self.mm(o, Mg(g8), U, True, False, rr, [self.rps[bank]], last=False)
                self.mm(o, WYr(g8), XS[:, gi * 256:(gi + 1) * 256], False, False, rr, [self.rps[bank]], last=False)
                self.mm(o, WYi(g8), XS[:, 1024 + gi * 256: 1024 + (gi + 1) * 256], False, True, rr, [self.rps[bank]], last=True)
                if gi % 2 == 1:
                    P.op("act", (lambda bank=bank, g8=g8: lambda e: e.copy(
                        out=Yq[:, (g8 - 1) * 256:(g8 + 1) * 256], in_=self.psv(bank * 512, (bank + 1) * 512)))(),
                        reads=[self.rps[bank]], writes=rYq)

        def UNSEL(q, part):
            for t0 in range(part * 4, part * 4 + 4):
                tb = (t0 % 4) // 2
                o = self.psv(tb * 512 + (t0 % 2) * 256, tb * 512 + (t0 % 2 + 1) * 256)
                for g8 in range(8):
                    self.mm(o, selb[:, t0 * 240 + (7 - g8) * 16: t0 * 240 + (7 - g8) * 16 + 128],
                            Yq[:, g8 * 256:(g8 + 1) * 256], g8 == 0, g8 == 7,
                            [self.r_selb] + rYq, [self.rps[tb]], last=(g8 == 7))

        def POST(q, part):
            TMP, rTMP = (Av, rAv) if part == 0 else (Bv, rBv)
            dcol = self.col("ssm_d", l, q)
            uperm = cap(self.bview(q), part * 4, [[1, 4], [8, 256]])
            tm3 = TMP.rearrange("p (t c) -> p t c", t=4)
            ps3 = self.psv(0, 1024).rearrange("p (t c) -> p t c", t=4)
            P.op("dve", lambda e: e.scalar_tensor_tensor(out=tm3, in0=uperm, scalar=dcol, in1=ps3, op0=ALU.mult, op1=ALU.add),
                 reads=[self.rB[q], self.r_colp] + self.rps[0:2], writes=rTMP)
            P.op("act", lambda e: e.activation(out=uperm, in_=tm3, func=AF.Gelu_apprx_tanh),
                 reads=rTMP, writes=[self.rB[q]])

        load_LZ(0)
        load_WYM(0)
        load_TAB(0, 0)
        SEL(0)
        Zmm(0, 0)
        for q in range(4):
            load_TAB(q, 1)
            Bst(q, 0)
            Zmm(q, 1)
            if q < 3:
                load_LZ(q + 1)
            if q >= 1:
                UNSEL(q - 1, 0)
                POST(q - 1, 0)
            if q < 3:
                SEL(q + 1)
            if q >= 1:
                UNSEL(q - 1, 1)
            Ymm(q, 0)
            if q < 3:
                load_TAB(q + 1, 0)
            Bst(q, 1)
            if q >= 1:
                POST(q - 1, 1)
            if q < 3:
                Zmm(q + 1, 0)
            Ymm(q, 1)
            if q < 3:
                load_WYM(q + 1)
        UNSEL(3, 0)
        POST(3, 0)
        UNSEL(3, 1)
        POST(3, 1)
        SG, rSG = self.pgv(8, 1, BF16), self.prs(8)

        def evac_glu(m, pv, prs):
            bcol = self.col("b_glu", l, m)
            P.op("act", lambda e: e.activation(out=SG, in_=pv, func=AF.Sigmoid, bias=bcol),
                 reads=list(prs) + [self.r_colp], writes=rSG)
            P.op("dve", lambda e: e.tensor_tensor(out=self.aview(m), in0=self.bview(m), in1=SG, op=ALU.mult),
                 reads=rSG + [self.rB[m]], writes=[self.rA[m]])
        self.fm_proj(self.w_glu[l], 0, 4, lambda k, tt: self.bview(k, tt * 512, (tt + 1) * 512),
                     lambda k: [self.rB[k]], evac_glu, nk=4)

    def final_out(self, b):
        P = self.P
        fg_bc = self.pgv(13)
        r_fg = self.prs(13)
        P.dma("sp", lambda e: e.dma_start(out=fg_bc, in_=self.final_g.partition_broadcast(128)), "d_c2", writes=r_fg)
        for n in range(16):
            ot = self.pgv(11 + (n % 2))
            rot = self.rpg[11 + (n % 2)]
            ss = self.small.ap()[:, 8 + 4 * (n % 2): 8 + 4 * (n % 2) + 4]
            banks = [(2 * n) % 8, (2 * n + 1) % 8]
            for half in range(2):
                bank = banks[half]
                for cc in range(4):
                    c = half * 4 + cc
                    o = self.psv(bank * 512 + cc * 128, bank * 512 + (cc + 1) * 128)
                    i_ = self.hview(c, n * 128, (n + 1) * 128)
                    P.op("pe", (lambda o=o, i_=i_: lambda e: e.transpose(out=o, in_=i_, identity=self.ident.ap()))(),
                         reads=[self.rH[c], self.r_ident], writes=[self.rps[bank]], milestone=(cc == 3), skip_self=True)
                P.op("act", (lambda bank=bank, half=half, ot=ot, ss=ss: lambda e: e.activation(
                    out=ot[:, half * 512:(half + 1) * 512], in_=self.psv(bank * 512, (bank + 1) * 512),
                    func=AF.Square, accum_out=ss[:, half:half + 1]))(),
                    reads=[self.rps[bank]], writes=[rot, self.r_small])
            P.op("dve", (lambda ss=ss: lambda e: e.tensor_tensor(out=ss[:, 2:3], in0=ss[:, 0:1], in1=ss[:, 1:2], op=ALU.add))(),
                 reads=[self.r_small], writes=[self.r_small])
            P.op("act", (lambda ss=ss: lambda e: e.activation(out=ss[:, 3:4], in_=ss[:, 2:3], func=AF.Ln,
                                                              scale=1.0 / D_MODEL, bias=self.eps_ap))(),
                 reads=[self.r_small], writes=[self.r_small])
            P.op("act", (lambda ss=ss: lambda e: e.activation(out=ss[:, 3:4], in_=ss[:, 3:4], func=AF.Exp, scale=-0.5))(),
                 reads=[self.r_small], writes=[self.r_small])
            for half in range(2):
                bank = banks[half]
                P.op("dve", (lambda bank=bank, half=half, ot=ot, ss=ss: lambda e: e.scalar_tensor_tensor(
                    out=ot[:, half * 512:(half + 1) * 512], in0=self.psv(bank * 512, (bank + 1) * 512),
                    scalar=ss[:, 3:4], in1=fg_bc[:, half * 512:(half + 1) * 512],
                    op0=ALU.mult, op1=ALU.mult))(),
                    reads=[self.rps[bank], self.r_small, rot] + r_fg, writes=[rot])
            P.dma("sp", (lambda ot=ot, n=n: lambda e: e.dma_start(out=self.out[b, n * 128:(n + 1) * 128, :], in_=ot))(),
                  "d_out%d" % (n % 2), reads=[rot], writes=[self.r_out])

    def build(self):
        cfg, P = self.cfg, self.P
        self.r_out = Reg("out")
        self.pb_n = 0
        self.pb2_n = 0
        self.sq_ready = False
        self.consts()
        P.op("dve", lambda e: e.memset(self.small.ap(), 0.0), writes=[self.r_small])
        P.op("dve", lambda e: e.memset(self.small.ap()[:, 0:1], EPS), writes=[self.r_small])
        self.eps_ap = self.small.ap()[:, 0:1]
        if cfg.mixer is True or "s5" in cfg.dumps:
            for l in range(cfg.depth):
                self.s5_prologue(l)
        for b in range(cfg.nseq):
            self.load_x(b)
            for l in range(cfg.depth):
                if cfg.mixer:
                    self.mixer(b, l)
                if cfg.xattn:
                    self.xattn(b, l)
                if cfg.ffn:
                    self.ffn(l)
            self.final_out(b)
        P.wait_all("sp", [self.r_out] + [r for rs in getattr(self, "r_s5w", {}).values() for r in rs] + list(getattr(self, "r_tab", {}).values()))
        P.emit()
        P.close()
        print("[kernel] ops per engine:", {e: len(P.ops[e]) for e in ENGS}, "sbuf left", self.nc.sbuf_bytes_remaining)


def host_prep(inputs, cfg):
    d = cfg.depth
    off, ncol = colp_layout(d)
    colp = np.zeros((128, ncol), np.float32)

    def put(nm, l, vec):
        v = np.asarray(vec, np.float32).reshape(-1, 128).T
        colp[:, off[(nm, l)]: off[(nm, l)] + v.shape[1]] = v
    for l in range(d):
        put("g_mix", l, inputs["norm_mix_g"][l])
        put("g_x", l, inputs["norm_xattn_g"][l])
        put("g_f", l, inputs["norm_ffn_g"][l])
        put("ssm_d", l, inputs["ssm_d"][l])
        put("b_glu", l, inputs["b_glu"][l])
        put("cw0", l, inputs["conv_w"][l, 0])
        put("cw1", l, inputs["conv_w"][l, 1])
        put("cw2", l, inputs["conv_w"][l, 2])
        put("cb", l, inputs["conv_b"][l])
    shared = {
        "colp": colp,
        "final_g": np.ascontiguousarray(np.asarray(inputs["final_g"], np.float32).reshape(1, D_MODEL)),
        "ident": np.eye(128, dtype=np.float32),
        "w_up": np.ascontiguousarray(np.asarray(inputs["w_up"], np.float32)[:d]),
        "w_q": np.ascontiguousarray(np.asarray(inputs["w_q"], np.float32)[:d]),
        "w_in": np.ascontiguousarray(np.asarray(inputs["w_in"], np.float32)[:d]),
        "w_out": np.ascontiguousarray(np.asarray(inputs["w_out"], np.float32)[:d]),
        "gv": np.ascontiguousarray(np.asarray(inputs["gmlp_norm_g"], np.float32)[:d]),
        "bs": np.ascontiguousarray(np.asarray(inputs["gmlp_b_s"], np.float32)[:d].reshape(d, 512)),
        "wsT": np.ascontiguousarray(np.asarray(inputs["gmlp_w_s"], np.float32)[:d].transpose(0, 3, 1, 2).reshape(d, 128, 512)),
        "w_kv": np.ascontiguousarray(np.asarray(inputs["w_kv"], np.float32)[:d]),
        "w_o": np.ascontiguousarray(np.asarray(inputs["w_o"], np.float32)[:d]),
        "mem_g": np.ascontiguousarray(np.asarray(inputs["mem_norm_g"], np.float32)[:d]),
        "w_down": np.ascontiguousarray(np.asarray(inputs["w_down"], np.float32)[:d]),
    }
    tr = lambda a, perm: np.asarray(a, np.float32)[:d].transpose(perm)
    a_re = tr(inputs["ssm_a_re"], (0, 1, 3, 2)).reshape(d, 128, 32)
    a_im = tr(inputs["ssm_a_im"], (0, 1, 3, 2)).reshape(d, 128, 32)
    ldt = np.repeat(np.asarray(inputs["ssm_log_dt"], np.float32)[:d, :, None, :], 64, axis=2).reshape(d, 128, 32)
    shared["s5p"] = np.ascontiguousarray(np.concatenate([a_re, a_im, ldt], axis=2))
    b_re = tr(inputs["ssm_b_re"], (0, 1, 3, 2, 4)).reshape(d, 128, 512)
    b_im = tr(inputs["ssm_b_im"], (0, 1, 3, 2, 4)).reshape(d, 128, 512)
    shared["s5b"] = np.ascontiguousarray(np.concatenate([b_re, b_im], axis=2))
    c_re = tr(inputs["ssm_c_re"], (0, 1, 4, 2, 3)).reshape(d, 128, 512)
    c_im = tr(inputs["ssm_c_im"], (0, 1, 4, 2, 3)).reshape(d, 128, 512)
    shared["s5c"] = np.ascontiguousarray(np.concatenate([c_re, c_im], axis=2))
    shared["w_glu"] = np.ascontiguousarray(np.asarray(inputs["w_glu"], np.float32)[:d])
    k = np.zeros((128, 1024), np.float32)
    j = np.arange(8, dtype=np.float32)
    k[:64, 0:8] = 7 - j; k[64:, 0:8] = j
    k[:64, 8:16] = -1 - j; k[64:, 8:16] = j - 8
    k[:64, 16:24] = j + 1; k[64:, 16:24] = 8 - j
    k[:, 24] = 8.0; k[:, 25] = 1.0
    k[:64, 26] = 1.0; k[64:, 26] = -1.0
    jj = np.arange(256, dtype=np.float32)
    k[:, 32:288] = jj
    k[:, 288:544] = 1.0; k[:64, 288] = 0.0; k[64:, 543] = 0.0
    k[:64, 800] = 1.0; k[64:, 801] = 1.0
    k[:, 832:848] = 16.0 * np.arange(16, dtype=np.float32); k[:, 848:864] = np.arange(16, dtype=np.float32)
    sidx = np.arange(128) // 16
    k[:, 544:672] = (sidx[:, None] <= sidx[None, :]).astype(np.float32)
    k[:, 672:800] = (sidx[:, None] >= sidx[None, :]).astype(np.float32)
    shared["s5k"] = k
    selb = np.zeros((128, 8, 240), np.float32)
    for a in range(8):
        for h in range(16):
            selb[a * 16 + h, a, 112 + h] = 1.0
    shared["selb"] = selb.reshape(128, 8 * 240)
    return shared


def run(inputs, cfg, ncores=NCORES):
    nc = bass.Bass("TRN2", target_bir_lowering=False)
    bld = Builder(nc, cfg)
    bld.build()
    shared = host_prep(inputs, cfg)
    x = np.asarray(inputs["x"], np.float32)
    mem = np.asarray(inputs["mem"], np.float32)
    in_maps = []
    for c in range(ncores):
        m = dict(shared)
        m["x"] = np.ascontiguousarray(x[c * cfg.nseq:(c + 1) * cfg.nseq])
        m["mem"] = np.ascontiguousarray(mem[c * cfg.nseq:(c + 1) * cfg.nseq])
        in_maps.append(m)
    res = run_bass_kernel_spmd(nc, in_maps, core_ids=list(range(ncores)))
    return res


def kernel(**inputs):
    cfg = Cfg()
    res = run(inputs, cfg)
    out = np.concatenate([np.asarray(r["out"], np.float32) for r in res.results], axis=0)
    return out
```

```python
import math
import numpy as np
import concourse.bass as bass
import concourse.mybir as mybir
from concourse.bass_utils import run_bass_kernel_spmd
from concourse.ap import AP

F32 = mybir.dt.float32
BF16 = mybir.dt.bfloat16
ALU = mybir.AluOpType
AF = mybir.ActivationFunctionType
AX = mybir.AxisListType

D_MODEL = 1024
SEQ = 2048
N_MEM = 256
D_FF = 2816
NFF = 22
DEPTH = 4
EPS = 1e-6
NCORES = 8
ENGS = ("pe", "act", "dve", "pool", "sp")


class Reg:
    __slots__ = ("name", "w", "rs")

    def __init__(self, name=""):
        self.name = name
        self.w = None
        self.rs = []


class Prog:
    def __init__(self, nc):
        self.nc = nc
        self.ops = {e: [] for e in ENGS}
        self.cnt = {e: 0 for e in ENGS}
        self.seen = {e: {} for e in ENGS}
        self.sems = {}
        self.dma_cnt = {}
        self._ctx = []
        for e in ENGS:
            self._sem("eng_" + e)

    def _sem(self, key):
        if key not in self.sems:
            cm = self.nc.semaphore(key)
            h = cm.__enter__()
            self._ctx.append(cm)
            self.sems[key] = h
        return self.sems[key]

    def _deps(self, eng, reads, writes, skip_self):
        toks = []
        for r in reads:
            if r.w is not None:
                toks.append(r.w)
        for w in writes:
            if w.w is not None:
                toks.append(w.w)
            toks.extend(w.rs)
        waits = {}
        own = "eng_" + eng
        for (k, v) in toks:
            if k == own and (skip_self or v > self.cnt[eng]):
                continue
            if self.seen[eng].get(k, 0) >= v:
                continue
            waits[k] = max(waits.get(k, 0), v)
        for k, v in waits.items():
            self.seen[eng][k] = v
        return list(waits.items())

    def op(self, eng, fn, reads=(), writes=(), milestone=True, skip_self=False):
        waits = self._deps(eng, reads, writes, skip_self)
        own = "eng_" + eng
        if milestone:
            self.cnt[eng] += 1
        tok = (own, self.cnt[eng] if milestone else self.cnt[eng] + 1)
        self.ops[eng].append((waits, fn, (own, 1) if milestone else None))
        for r in reads:
            r.rs.append(tok)
            if len(r.rs) > 64:
                r.rs = _prune(r.rs)
        for w in writes:
            w.w = tok
            w.rs = []
        return tok

    def dma(self, eng, fn, semkey, reads=(), writes=(), n=1):
        self._sem(semkey)
        waits = self._deps(eng, reads, writes, False)
        self.dma_cnt[semkey] = self.dma_cnt.get(semkey, 0) + 16 * n
        tok = (semkey, self.dma_cnt[semkey])
        self.ops[eng].append((waits, fn, (semkey, 16)))
        for r in reads:
            r.rs.append(tok)
            if len(r.rs) > 64:
                r.rs = _prune(r.rs)
        for w in writes:
            w.w = tok
            w.rs = []
        return tok

    def wait_all(self, eng, regs):
        toks = []
        for r in regs:
            if r.w is not None:
                toks.append(r.w)
            toks.extend(r.rs)
        waits = {}
        for (k, v) in toks:
            if self.seen[eng].get(k, 0) >= v:
                continue
            waits[k] = max(waits.get(k, 0), v)
        for k, v in waits.items():
            self.seen[eng][k] = v
        self.ops[eng].append((list(waits.items()), None, None))

    def emit(self):
        nc = self.nc
        with nc.Block() as block:
            def mk(e):
                def body(engobj):
                    for (waits, fn, inc) in self.ops[e]:
                        for (k, v) in waits:
                            engobj.wait_ge(self.sems[k], v)
                        if fn is None:
                            continue
                        ins = fn(engobj)
                        if inc is not None:
                            if isinstance(ins, (list, tuple)):
                                for i in ins:
                                    i.then_inc(self.sems[inc[0]], inc[1])
                            else:
                                ins.then_inc(self.sems[inc[0]], inc[1])
                return body
            block.tensor(mk("pe"))
            block.scalar(mk("act"))
            block.vector(mk("dve"))
            block.gpsimd(mk("pool"))
            block.sync(mk("sp"))

    def close(self):
        for cm in reversed(self._ctx):
            cm.__exit__(None, None, None)


def _prune(toks):
    best = {}
    for (k, v) in toks:
        if best.get(k, 0) < v:
            best[k] = v
    return list(best.items())


def cap(base, off, dims):
    return AP(base.tensor, base.offset + off, [list(base.ap[0])] + [list(d) for d in dims])


class Cfg:
    def __init__(self, depth=DEPTH, nseq=2, mixer=True, xattn=True, ffn=True, dumps=()):
        self.depth = depth
        self.nseq = nseq
        self.mixer = mixer
        self.xattn = xattn
        self.ffn = ffn
        self.dumps = tuple(dumps)


def colp_layout(depth):
    off = {}
    n = 0
    for l in range(depth):
        for nm, w in (("g_mix", 8), ("g_x", 8), ("g_f", 8), ("ssm_d", 4), ("b_glu", 4),
                      ("cw0", 44), ("cw1", 44), ("cw2", 44), ("cb", 44)):
            off[(nm, l)] = n
            n += w
    return off, n


class Builder:
    def __init__(self, nc, cfg):
        self.nc = nc
        self.cfg = cfg
        self.P = Prog(nc)
        self.dump_out = {}
        self._uid = 0
        self.declare_io()
        self.alloc()

    def sb(self, name, cols, dtype):
        return self.nc.alloc_sbuf_tensor(name, [128, cols], dtype)

    def declare_io(self):
        nc, cfg = self.nc, self.cfg
        d = cfg.depth
        di = lambda name, shape: nc.dram_tensor(name, list(shape), F32, kind="ExternalInput").ap()
        self.x = di("x", (cfg.nseq, SEQ, D_MODEL))
        self.mem = di("mem", (cfg.nseq, N_MEM, D_MODEL))
        self.colp_off, ncol = colp_layout(d)
        self.colp_d = di("colp", (128, ncol))
        self.final_g = di("final_g", (1, D_MODEL))
        self.ident_d = di("ident", (128, 128))
        self.w_up = di("w_up", (d, D_MODEL, 2 * D_FF))
        self.w_down = di("w_down", (d, D_FF, D_MODEL))
        self.w_q = di("w_q", (d, D_MODEL, D_MODEL))
        self.w_kv = di("w_kv", (d, D_MODEL, 2 * D_MODEL))
        self.w_o = di("w_o", (d, D_MODEL, D_MODEL))
        self.mem_g = di("mem_g", (d, D_MODEL))
        self.w_in = di("w_in", (d, D_MODEL, 1536))
        self.w_out = di("w_out", (d, D_MODEL, D_MODEL))
        self.gv = di("gv", (d, 512))
        self.bs = di("bs", (d, 512))
        self.wsT = di("wsT", (d, 128, 512))
        self.w_glu = di("w_glu", (d, 512, 512))
        self.s5p = di("s5p", (d, 128, 96))
        self.s5b = di("s5b", (d, 128, 1024))
        self.s5c = di("s5c", (d, 128, 1024))
        self.s5k = di("s5k", (128, 1024))
        self.selb_d = di("selb", (128, 8 * 240))
        knd = "ExternalOutput" if "s5" in cfg.dumps else "Internal"
        self.s5w_d = nc.dram_tensor("s5w_scr", [d, 4, 128, 5120], BF16, kind=knd).ap()
        self.tab_d = nc.dram_tensor("s5t_scr", [d, 8, 128, 3072], F32, kind=knd).ap()
        self.tabb_d = nc.dram_tensor("s5tb_scr", [d, 8, 128, 2048], BF16).ap()
        self.rho_d = nc.dram_tensor("s5rho_scr", [d, 8, 128, 1024], F32).ap()
        self.out = nc.dram_tensor("out", [cfg.nseq, SEQ, D_MODEL], F32, kind="ExternalOutput").ap()

    def alloc(self):
        nc = self.nc
        self.H = self.sb("H", 8 * SEQ, F32)
        self.rH = [Reg("H%d" % c) for c in range(8)]
        self.A = self.sb("A", 8 * SEQ, BF16)
        self.rA = [Reg("A%d" % c) for c in range(8)]
        self.B = self.sb("B", 8 * SEQ, BF16)
        self.rB = [Reg("B%d" % c) for c in range(8)]
        self.NR = 3
        self.ring = [self.sb("ring%d" % i, 2048, BF16) for i in range(self.NR)]
        self.rring = [Reg("ring%d" % i) for i in range(self.NR)]
        self.ring_n = 0
        self.NPG = 14
        self.arena = self.sb("arena", 1024 * self.NPG, F32)
        self.rpg = [Reg("pg%d" % i) for i in range(self.NPG)]
        self.colp = self.sb("colp_sb", self.colp_d.shape[1], F32)
        self.r_colp = Reg("colp")
        self.ident = self.sb("ident_sb", 128, F32)
        self.r_ident = Reg("ident")
        self.ones_bf = self.sb("ones_bf", 128, BF16)
        self.r_ones = Reg("ones")
        self.ident_bf = self.sb("ident_bf", 128, BF16)
        self.selb = self.sb("selb_sb", 8 * 240, BF16)
        self.r_selb = Reg("selb")
        self.small = self.sb("small", 80, F32)
        self.r_small = Reg("small")
        self.ps = nc.alloc_psum_tensor("ps", [128, 4096], F32)
        self.rps = [Reg("ps%d" % i) for i in range(8)]

    def pgv(self, p0, n=1, dtype=F32):
        a = self.arena.ap()[:, p0 * 1024:(p0 + n) * 1024]
        return a if dtype == F32 else a.bitcast(dtype)

    def prs(self, p0, n=1):
        return self.rpg[p0:p0 + n]

    def hview(self, c, lo=0, hi=SEQ):
        return self.H.ap()[:, c * SEQ + lo: c * SEQ + hi]

    def aview(self, c, lo=0, hi=SEQ):
        return self.A.ap()[:, c * SEQ + lo: c * SEQ + hi]

    def bview(self, c, lo=0, hi=SEQ):
        return self.B.ap()[:, c * SEQ + lo: c * SEQ + hi]

    def psv(self, lo, hi):
        return self.ps.ap()[:, lo:hi]

    def col(self, name, l, j=0, n=1):
        o = self.colp_off[(name, l)] + j
        return self.colp.ap()[:, o:o + n]

    def mm(self, out, lhsT, rhs, start, stop, reads, writes, last):
        self.P.op("pe", lambda e: e.matmul(out, lhsT=lhsT, rhs=rhs, start=start, stop=stop),
                  reads=reads, writes=writes, milestone=last, skip_self=True)

    def ring_next(self):
        i = self.ring_n % self.NR
        self.ring_n += 1
        return self.ring[i], self.rring[i], "d_ring%d" % i

    def consts(self):
        P = self.P
        P.dma("sp", lambda e: e.dma_start(out=self.colp.ap(), in_=self.colp_d), "d_c0", writes=[self.r_colp])
        P.dma("sp", lambda e: e.dma_start(out=self.ident.ap(), in_=self.ident_d), "d_c1", writes=[self.r_ident])
        P.op("dve", lambda e: e.memset(self.ones_bf.ap(), 1.0), writes=[self.r_ones])
        P.dma("pool", lambda e: e.dma_start(out=self.selb.ap(), in_=self.selb_d), "d_c3", writes=[self.r_selb])
        P.op("dve", lambda e: e.tensor_copy(out=self.ident_bf.ap(), in_=self.ident.ap()), reads=[self.r_ident], writes=[self.r_ident])

    def load_x(self, b):
        P = self.P
        for n in range(16):
            xt = self.pgv(9 + n % 2)
            rxt = self.rpg[9 + n % 2]
            P.dma("sp", (lambda xt=xt, n=n: lambda e: e.dma_start(out=xt, in_=self.x[b, n * 128:(n + 1) * 128, :]))(),
                  "d_xt%d" % (n % 2), writes=[rxt])
            for half in range(2):
                bank = (2 * n + half) % 8
                for cc in range(4):
                    c = half * 4 + cc
                    o = self.psv(bank * 512 + cc * 128, bank * 512 + (cc + 1) * 128)
                    i_ = xt[:, c * 128:(c + 1) * 128]
                    P.op("pe", (lambda o=o, i_=i_: lambda e: e.transpose(out=o, in_=i_, identity=self.ident.ap()))(),
                         reads=[rxt, self.r_ident], writes=[self.rps[bank]], milestone=(cc == 3), skip_self=True)
                src = self.psv(bank * 512, (bank + 1) * 512).rearrange("p (c t) -> p c t", c=4)
                dst = cap(self.H.ap(), half * 4 * SEQ + n * 128, [[SEQ, 4], [1, 128]])
                P.op("act", (lambda src=src, dst=dst: lambda e: e.copy(out=dst, in_=src))(),
                     reads=[self.rps[bank]], writes=[self.rH[half * 4 + cc] for cc in range(4)])

    def rmsnorm_fm(self, gname, l):
        P = self.P
        sq = [self.pgv(2, 1, BF16), self.pgv(3, 1, BF16)]
        rsq = [self.rpg[2], self.rpg[3]]
        presummed = getattr(self, "sq_ready", False)
        self.sq_ready = False
        sb0 = 4 if presummed else 0
        for c in range(0 if presummed else 8):
            s, rs = sq[c % 2], rsq[c % 2]
            sqv = s[:, 0:SEQ]
            if c % 2 == 0:
                P.op("act", (lambda sqv=sqv, c=c: lambda e: e.activation(out=sqv, in_=self.hview(c), func=AF.Square))(),
                     reads=[self.rH[c]], writes=[rs])
            else:
                P.op("dve", (lambda sqv=sqv, c=c: lambda e: e.tensor_tensor(out=sqv, in0=self.hview(c), in1=self.hview(c), op=ALU.mult))(),
                     reads=[self.rH[c]], writes=[rs])
            for tt in range(4):
                self.mm(self.psv(tt * 512, (tt + 1) * 512), self.ones_bf.ap(), sqv[:, tt * 512:(tt + 1) * 512],
                        c == 0, c == 7, [rs, self.r_ones], [self.rps[tt]], last=(tt == 3))
        rstd = self.pgv(0, 2)
        r_rstd = self.prs(0, 2)
        P.op("act", lambda e: e.activation(out=rstd, in_=self.psv(sb0 * 512, sb0 * 512 + SEQ), func=AF.Ln, scale=1.0 / D_MODEL, bias=self.eps_ap),
             reads=self.rps[sb0:sb0 + 4] + [self.r_small], writes=r_rstd)
        P.op("act", lambda e: e.activation(out=rstd, in_=rstd, func=AF.Exp, scale=-0.5),
             reads=r_rstd, writes=r_rstd)
        for c in range(8):
            g = self.col(gname, l, c)
            if False:
                tmpn = self.pgv(4, 2)
                P.op("pool", (lambda c=c: lambda e: e.tensor_tensor(out=tmpn, in0=self.hview(c), in1=rstd, op=ALU.mult))(),
                     reads=[self.rH[c]] + r_rstd, writes=self.prs(4, 2))
                P.op("pool", (lambda c=c, g=g: lambda e: e.tensor_scalar(out=self.aview(c), in0=tmpn, scalar1=g, scalar2=None, op0=ALU.mult))(),
                     reads=self.prs(4, 2) + [self.r_colp], writes=[self.rA[c]])
            else:
                P.op("dve", (lambda c=c, g=g: lambda e: e.scalar_tensor_tensor(
                    out=self.aview(c), in0=self.hview(c), scalar=g, in1=rstd, op0=ALU.mult, op1=ALU.mult))(),
                    reads=[self.rH[c], self.r_colp] + r_rstd, writes=[self.rA[c]])

    def ffn(self, l):
        P = self.P
        self.rmsnorm_fm("g_f", l)
        thirds = [(0, 8), (8, 15), (15, 22)]
        tg, tv, gg = self.pgv(4, 2), self.pgv(6, 2), self.pgv(8, 1, BF16)
        r_tg, r_tv, r_gg = self.prs(4, 2), self.prs(6, 2), self.prs(8, 1)
        tvb, r_tvb = self.pgv(9, 1, BF16), self.prs(9)
        for (j0, j1) in thirds:
            for j in range(j0, j1):
                slot, rslot, sk = self.ring_next()
                sv = slot.ap().rearrange("p (k m) -> p k m", k=16)
                srcg = self.w_up[l, :, j * 128:(j + 1) * 128].rearrange("(k p) m -> p k m", p=128)
                srcv = self.w_up[l, :, D_FF + j * 128: D_FF + (j + 1) * 128].rearrange("(k p) m -> p k m", p=128)
                P.dma("pool", (lambda sv=sv, srcg=srcg, srcv=srcv: lambda e: [
                    e.dma_start(out=sv[:, 0:8, :], in_=srcg), e.dma_start(out=sv[:, 8:16, :], in_=srcv)])(),
                    sk, writes=[rslot], n=2)
                for which in range(2):
                    pb = which * 4
                    if j == 0 and which == 0:
                        for k in range(8):
                            for tt in range(4):
                                self.mm(self.psv((pb + tt) * 512, (pb + tt + 1) * 512), sv[:, which * 8 + k, :],
                                        self.aview(k, tt * 512, (tt + 1) * 512), k == 0, k == 7,
                                        [rslot, self.rA[k]], [self.rps[pb + tt]], last=True)
                    else:
                        for tt in range(4):
                            for k in range(8):
                                self.mm(self.psv((pb + tt) * 512, (pb + tt + 1) * 512), sv[:, which * 8 + k, :],
                                        self.aview(k, tt * 512, (tt + 1) * 512), k == 0, k == 7,
                                        [rslot, self.rA[k]], [self.rps[pb + tt]], last=(k == 7))
                    jc = j + which * NFF
                    t = tg if which == 0 else tv
                    rt = r_tg if which == 0 else r_tv
                    pv = self.psv(pb * 512, pb * 512 + SEQ)
                    rp = self.rps[pb:pb + 4]
                    w0, w1, w2, cb = (self.col("cw0", l, jc), self.col("cw1", l, jc), self.col("cw2", l, jc),
                                      self.col("cb", l, jc))
                    P.op("act", (lambda t=t, pv=pv, w1=w1, cb=cb: lambda e: e.activation(
                        out=t, in_=pv, func=AF.Identity, scale=w1, bias=cb))(),
                        reads=rp + [self.r_colp], writes=rt)
                    P.op("dve", (lambda t=t, pv=pv, w0=w0: lambda e: e.scalar_tensor_tensor(
                        out=t[:, 1:SEQ], in0=pv[:, 0:SEQ - 1], scalar=w0, in1=t[:, 1:SEQ],
                        op0=ALU.mult, op1=ALU.add))(), reads=rp + [self.r_colp] + rt, writes=rt)
                    if which == 0:
                        P.op("dve", (lambda t=t, pv=pv, w2=w2: lambda e: e.scalar_tensor_tensor(
                            out=t[:, 0:SEQ - 1], in0=pv[:, 1:SEQ], scalar=w2, in1=t[:, 0:SEQ - 1],
                            op0=ALU.mult, op1=ALU.add))(), reads=rp + [self.r_colp] + rt, writes=rt)
                    else:
                        P.op("dve", (lambda t=t, pv=pv, w2=w2: lambda e: e.scalar_tensor_tensor(
                            out=tvb[:, 0:SEQ - 1], in0=pv[:, 1:SEQ], scalar=w2, in1=t[:, 0:SEQ - 1],
                            op0=ALU.mult, op1=ALU.add))(), reads=rp + [self.r_colp] + rt, writes=r_tvb)
                        P.op("act", (lambda t=t: lambda e: e.copy(out=tvb[:, SEQ - 1:SEQ], in_=t[:, SEQ - 1:SEQ]))(),
                             reads=rt, writes=r_tvb)
                    if which == 0:
                        P.op("act", lambda e: e.activation(out=gg, in_=tg, func=AF.Gelu_apprx_tanh),
                             reads=r_tg, writes=r_gg)
                jj = j - j0
                P.op("dve", (lambda jj=jj: lambda e: e.tensor_tensor(out=self.bview(jj), in0=gg, in1=tvb, op=ALU.mult))(),
                     reads=r_gg + r_tvb, writes=[self.rB[jj]])
            nk = j1 - j0
            for m in range(8):
                slot, rslot, sk = self.ring_next()
                sv = slot.ap()[:, 0:nk * 128].rearrange("p (k m) -> p k m", k=nk)
                src = self.w_down[l, j0 * 128:j1 * 128, m * 128:(m + 1) * 128].rearrange("(k p) m -> p k m", p=128)
                P.dma("pool", (lambda sv=sv, src=src: lambda e: e.dma_start(out=sv, in_=src))(), sk, writes=[rslot])
                last_third = (j1 == NFF)
                want_sq = last_third and bool(self.cfg.mixer) and (l + 1 < self.cfg.depth)
                self.res_chunk(m, (lambda kk, sv=sv: sv[:, kk, :]), [rslot], nk,
                               (lambda kk, tt: self.bview(kk, tt * 512, (tt + 1) * 512)), (lambda kk: [self.rB[kk]]), want_sq)
            if j1 == NFF and bool(self.cfg.mixer) and (l + 1 < self.cfg.depth):
                self.sq_ready = True

    def fm_proj(self, w2d, col0, nmc, rhs_of, rhs_regs, evac, nk=8, kouter_first=False):
        P = self.P
        m = 0
        while m < nmc:
            npair = min(2, nmc - m)
            slot, rslot, sk = self.ring_next()
            sv = slot.ap()[:, 0:nk * npair * 128].rearrange("p (k m) -> p k m", k=nk)
            src = w2d[:, col0 + m * 128: col0 + (m + npair) * 128].rearrange("(k p) m -> p k m", p=128)
            P.dma("pool", (lambda sv=sv, src=src: lambda e: e.dma_start(out=sv, in_=src))(), sk, writes=[rslot])
            for mi in range(npair):
                pb = (self.pb_n % 2) * 4
                self.pb_n += 1
                if kouter_first and m + mi == 0:
                    for k in range(nk):
                        for tt in range(4):
                            self.mm(self.psv((pb + tt) * 512, (pb + tt + 1) * 512), sv[:, k, mi * 128:(mi + 1) * 128],
                                    rhs_of(k, tt), k == 0, k == nk - 1, [rslot] + rhs_regs(k), [self.rps[pb + tt]],
                                    last=True)
                else:
                    for tt in range(4):
                        for k in range(nk):
                            self.mm(self.psv((pb + tt) * 512, (pb + tt + 1) * 512), sv[:, k, mi * 128:(mi + 1) * 128],
                                    rhs_of(k, tt), k == 0, k == nk - 1, [rslot] + rhs_regs(k), [self.rps[pb + tt]],
                                    last=(k == nk - 1))
                evac(m + mi, self.psv(pb * 512, pb * 512 + SEQ), self.rps[pb:pb + 4])
            m += npair

    def fm_proj_res(self, w2d, rhs_of, rhs_regs, nk=8, accumulate_sq=True):
        P = self.P
        m = 0
        while m < 8:
            slot, rslot, sk = self.ring_next()
            sv = slot.ap()[:, 0:nk * 256].rearrange("p (k m) -> p k m", k=nk)
            src = w2d[:, m * 128:(m + 2) * 128].rearrange("(k p) m -> p k m", p=128)
            P.dma("pool", (lambda sv=sv, src=src: lambda e: e.dma_start(out=sv, in_=src))(), sk, writes=[rslot])
            for mi in range(2):
                self.res_chunk(m + mi, lambda k, mi=mi: sv[:, k, mi * 128:(mi + 1) * 128], [rslot], nk, rhs_of, rhs_regs,
                               accumulate_sq)
            m += 2
        if accumulate_sq:
            self.sq_ready = True

    def flush_sq(self):
        pend = getattr(self, "_pend_sq", None)
        if pend is None:
            return
        m, sqv, rs = pend
        self._pend_sq = None
        for tt in range(4):
            self.mm(self.psv((4 + tt) * 512, (5 + tt) * 512), self.ones_bf.ap(), sqv[:, tt * 512:(tt + 1) * 512],
                    m == 0, m == 7, [rs, self.r_ones], [self.rps[4 + tt]], last=(tt == 3))

    def res_chunk(self, m, lhs_of, lhs_regs, nk, rhs_of, rhs_regs, accumulate_sq):
        P = self.P
        for half in range(2):
            g = self.pb2_n % 2
            self.pb2_n += 1
            for t2 in range(2):
                tt = half * 2 + t2
                bank = 2 * g + t2
                for k in range(nk):
                    self.mm(self.psv(bank * 512, (bank + 1) * 512), lhs_of(k), rhs_of(k, tt), k == 0, k == nk - 1,
                            list(lhs_regs) + rhs_regs(k), [self.rps[bank]], last=(k == nk - 1))
            if half == 1:
                self.flush_sq()
            pv = self.psv(2 * g * 512, (2 * g + 2) * 512)
            hv = self.hview(m, half * 1024, (half + 1) * 1024)
            P.op("dve", (lambda pv=pv, hv=hv: lambda e: e.tensor_tensor(out=hv, in0=pv, in1=hv, op=ALU.add))(),
                 reads=self.rps[2 * g:2 * g + 2] + [self.rH[m]], writes=[self.rH[m]])
        if accumulate_sq:
            sqv = self.pgv(2 + m % 2, 1, BF16)[:, 0:SEQ]
            rs = self.rpg[2 + m % 2]
            P.op("act", (lambda sqv=sqv, m=m: lambda e: e.activation(out=sqv, in_=self.hview(m), func=AF.Square))(),
                 reads=[self.rH[m]], writes=[rs])
            self._pend_sq = (m, sqv, rs)
            if m == 7:
                self.flush_sq()

    def resid_add(self, m, pv, prs):
        self.P.op("dve", lambda e: e.tensor_tensor(out=self.hview(m), in0=pv, in1=self.hview(m), op=ALU.add),
                  reads=list(prs) + [self.rH[m]], writes=[self.rH[m]])

    def xattn_kv(self, b, l):
        P = self.P
        sm = self.small.ap()
        memt, r_memt = self.pgv(4), self.prs(4)
        gbc, r_gbc = self.pgv(5), self.prs(5)
        memnT, r_memnT = self.pgv(6, 1, BF16), self.prs(6)
        KT, r_KT = self.pgv(7, 1, BF16), self.prs(7)
        V, r_V = self.pgv(8, 1, BF16), self.prs(8)
        P.dma("sp", lambda e: e.dma_start(out=gbc, in_=self.mem_g[l:l + 1, :].partition_broadcast(128)), "d_gbc", writes=r_gbc)
        for mt in range(2):
            P.dma("sp", (lambda mt=mt: lambda e: e.dma_start(out=memt, in_=self.mem[b, mt * 128:(mt + 1) * 128, :]))(),
                  "d_memt", writes=r_memt)
            junk = self.pgv(9)
            P.op("act", lambda e: e.activation(out=junk, in_=memt, func=AF.Square, accum_out=sm[:, 16:17]),
                 reads=r_memt, writes=self.prs(9) + [self.r_small])
            P.op("act", lambda e: e.activation(out=sm[:, 17:18], in_=sm[:, 16:17], func=AF.Ln, scale=1.0 / D_MODEL, bias=self.eps_ap),
                 reads=[self.r_small], writes=[self.r_small])
            P.op("act", lambda e: e.activation(out=sm[:, 17:18], in_=sm[:, 17:18], func=AF.Exp, scale=-0.5),
                 reads=[self.r_small], writes=[self.r_small])
            P.op("dve", lambda e: e.scalar_tensor_tensor(out=memt, in0=memt, scalar=sm[:, 17:18], in1=gbc, op0=ALU.mult, op1=ALU.mult),
                 reads=r_memt + r_gbc + [self.r_small], writes=r_memt)
            for half in range(2):
                bank = half
                for cc in range(4):
                    c = half * 4 + cc
                    o = self.psv(bank * 512 + cc * 128, bank * 512 + (cc + 1) * 128)
                    i_ = memt[:, c * 128:(c + 1) * 128]
                    P.op("pe", (lambda o=o, i_=i_: lambda e: e.transpose(out=o, in_=i_, identity=self.ident.ap()))(),
                         reads=r_memt + [self.r_ident], writes=[self.rps[bank]], milestone=(cc == 3), skip_self=True)
                src = self.psv(bank * 512, (bank + 1) * 512).rearrange("p (c t) -> p c t", c=4)
                dst = cap(memnT, half * 4 * 256 + mt * 128, [[256, 4], [1, 128]])
                P.op("act", (lambda src=src, dst=dst: lambda e: e.copy(out=dst, in_=src))(),
                     reads=[self.rps[bank]], writes=r_memnT)
        w2d = self.w_kv[l]
        for mp in range(4):
            slot, rslot, sk = self.ring_next()
            sv = slot.ap().rearrange("p (k m) -> p k m", k=8)
            src = w2d[:, mp * 256:(mp + 1) * 256].rearrange("(k p) m -> p k m", p=128)
            P.dma("pool", (lambda sv=sv, src=src: lambda e: e.dma_start(out=sv, in_=src))(), sk, writes=[rslot])
            bank = 2 + (mp % 2)
            for mi in range(2):
                for k in range(8):
                    self.mm(self.psv(bank * 512 + mi * 256, bank * 512 + (mi + 1) * 256), sv[:, k, mi * 128:(mi + 1) * 128],
                            memnT[:, k * 256:(k + 1) * 256], k == 0, k == 7, [rslot] + r_memnT, [self.rps[bank]],
                            last=(k == 7))
            P.op("act", (lambda bank=bank, mp=mp: lambda e: e.copy(out=KT[:, mp * 512:(mp + 1) * 512],
                                                                  in_=self.psv(bank * 512, (bank + 1) * 512)))(),
                 reads=[self.rps[bank]], writes=r_KT)
        for n2 in range(4):
            slot, rslot, sk = self.ring_next()
            sv = slot.ap().rearrange("p (k m) -> p k m", k=8)
            src = w2d[:, D_MODEL + n2 * 256: D_MODEL + (n2 + 1) * 256].rearrange("(k p) m -> p k m", p=128)
            P.dma("pool", (lambda sv=sv, src=src: lambda e: e.dma_start(out=sv, in_=src))(), sk, writes=[rslot])
            bank = 4 + (n2 % 2)
            for mt in range(2):
                for k in range(8):
                    self.mm(self.psv(bank * 512 + mt * 256, bank * 512 + (mt + 1) * 256),
                            memnT[:, k * 256 + mt * 128: k * 256 + (mt + 1) * 128], sv[:, k, :],
                            k == 0, k == 7, [rslot] + r_memnT, [self.rps[bank]], last=(k == 7))
            src_ps = self.psv(bank * 512, (bank + 1) * 512).rearrange("p (t m) -> p t m", t=2)
            dst = cap(V, n2 * 256, [[1024, 2], [1, 256]])
            P.op("act", (lambda src_ps=src_ps, dst=dst: lambda e: e.copy(out=dst, in_=src_ps))(),
                 reads=[self.rps[bank]], writes=r_V)
        self.kv_done = (b, l)

    def xattn(self, b, l):
        P = self.P
        sm = self.small.ap()
        if getattr(self, "kv_done", None) != (b, l):
            self.xattn_kv(b, l)
        KT, r_KT = self.pgv(7, 1, BF16), self.prs(7)
        V, r_V = self.pgv(8, 1, BF16), self.prs(8)
        self.rmsnorm_fm("g_x", l)

        def evac_q(m, pv, prs):
            P.op("act", lambda e: e.copy(out=self.bview(m), in_=pv), reads=list(prs), writes=[self.rB[m]])
        self.fm_proj(self.w_q[l], 0, 8, lambda k, tt: self.aview(k, tt * 512, (tt + 1) * 512),
                     lambda k: [self.rA[k]], evac_q, kouter_first=True)
        def bufs(n):
            par = n % 2
            return dict(par=par, sb0=2 * par, tb=4 + par,
                        Pm=self.pgv(9, 1, BF16)[:, par * 1024:(par + 1) * 1024],
                        PT=self.pgv(10, 1, BF16)[:, par * 1024:(par + 1) * 1024],
                        st=sm[:, 32 + par * 16: 32 + par * 16 + 16])
        if not hasattr(self, "r_st"):
            self.r_st = [Reg("st0"), Reg("st1")]
            self.r_PmPT = [[Reg("Pm0"), Reg("Pm1")], [Reg("PT0"), Reg("PT1")]]
        r_Pm, r_PT = self.r_PmPT
        P.op("dve", lambda e: e.memset(self.pgv(9, 1, BF16)[:, 0:2], 0.0), writes=self.prs(9) + r_Pm)
        P.op("dve", lambda e: e.memset(self.pgv(10, 1, BF16)[:, 0:2], 0.0), writes=self.prs(10) + r_PT)

        def stage_S1(n):
            bf = bufs(n)
            sb0, st = bf["sb0"], bf["st"]
            for h in range(4):
                bank = sb0 + h // 2
                o = self.psv(bank * 512 + (h % 2) * 256, bank * 512 + (h % 2 + 1) * 256)
                for dc in range(2):
                    c = 2 * h + dc
                    self.mm(o, self.bview(c, n * 128, (n + 1) * 128), KT[:, c * 256:(c + 1) * 256], dc == 0, dc == 1,
                            [self.rB[c]] + r_KT, [self.rps[bank]], last=(dc == 1))
            sc = self.psv(sb0 * 512, (sb0 + 2) * 512)
            sc3 = sc.rearrange("p (h m) -> p h m", h=4)
            r_sc = self.rps[sb0:sb0 + 2]
            P.op("dve", lambda e: e.tensor_reduce(out=st[:, 0:4], in_=sc3, axis=AX.X, op=ALU.max),
                 reads=r_sc, writes=[self.r_st[bf["par"]]])
            P.op("dve", lambda e: e.tensor_scalar(out=st[:, 4:8], in0=st[:, 0:4], scalar1=-1.0 / 16.0, scalar2=None, op0=ALU.mult),
                 reads=[self.r_st[bf["par"]]], writes=[self.r_st[bf["par"]]])

        def stage_S2(n):
            bf = bufs(n)
            sb0, Pm, st = bf["sb0"], bf["Pm"], bf["st"]
            rst = self.r_st[bf["par"]]
            sc = self.psv(sb0 * 512, (sb0 + 2) * 512)
            r_sc = self.rps[sb0:sb0 + 2]
            for h in range(4):
                P.op("act", (lambda h=h: lambda e: e.activation(
                    out=Pm[:, h * 256:(h + 1) * 256], in_=sc[:, h * 256:(h + 1) * 256], func=AF.Exp,
                    scale=1.0 / 16.0, bias=st[:, 4 + h:5 + h], accum_out=st[:, 8 + h:9 + h]))(),
                    reads=r_sc + [rst], writes=[r_Pm[bf["par"]], rst])

        def stage_S3(n):
            bf = bufs(n)
            Pm, st = bf["Pm"], bf["st"]
            rst = self.r_st[bf["par"]]
            P.op("dve", lambda e: e.reciprocal(out=st[:, 12:16], in_=st[:, 8:12]), reads=[rst], writes=[rst])
            Pm3 = Pm.rearrange("p (h m) -> p h m", h=4)
            rc3 = cap(st, 12, [[1, 4], [0, 256]])
            P.op("dve", lambda e: e.tensor_tensor(out=Pm3, in0=Pm3, in1=rc3, op=ALU.mult),
                 reads=[r_Pm[bf["par"]], rst], writes=[r_Pm[bf["par"]]])

        def stage_T(n):
            bf = bufs(n)
            tb, Pm, PT = bf["tb"], bf["Pm"], bf["PT"]
            tps = self.psv(tb * 512, (tb + 1) * 512).bitcast(BF16)
            for h in range(4):
                for mh in range(2):
                    j = h * 2 + mh
                    P.op("pe", (lambda j=j, h=h, mh=mh: lambda e: e.transpose(
                        out=tps[:, j * 128:(j + 1) * 128], in_=Pm[:, h * 256 + mh * 128: h * 256 + (mh + 1) * 128],
                        identity=self.ident_bf.ap()))(),
                        reads=[r_Pm[bf["par"]], self.r_ident], writes=[self.rps[tb]], milestone=(j == 7), skip_self=True)
            P.op("act", lambda e: e.copy(out=PT, in_=tps), reads=[self.rps[tb]], writes=[r_PT[bf["par"]]])

        def stage_PV(n):
            bf = bufs(n)
            PT = bf["PT"]
            ob = 6
            for h in range(4):
                for dc in range(2):
                    c = 2 * h + dc
                    bank = ob + c // 4
                    o = self.psv(bank * 512 + (c % 4) * 128, bank * 512 + (c % 4 + 1) * 128)
                    for mh in range(2):
                        self.mm(o, V[:, mh * 1024 + c * 128: mh * 1024 + (c + 1) * 128],
                                PT[:, (h * 2 + mh) * 128:(h * 2 + mh + 1) * 128], mh == 0, mh == 1,
                                r_V + [r_PT[bf["par"]]], [self.rps[bank]], last=(mh == 1))
            for half in range(2):
                bank = ob + half
                src = self.psv(bank * 512, (bank + 1) * 512).rearrange("p (c t) -> p c t", c=4)
                dst = cap(self.A.ap(), half * 4 * SEQ + n * 128, [[SEQ, 4], [1, 128]])
                P.op("dve", (lambda src=src, dst=dst: lambda e: e.tensor_copy(out=dst, in_=src))(),
                     reads=[self.rps[bank]], writes=[self.rA[half * 4 + cc] for cc in range(4)])

        for k in range(-2, 16):
            if 0 <= k + 2 < 16:
                stage_S1(k + 2)
                stage_S2(k + 2)
            if 0 <= k + 1 < 16:
                stage_S3(k + 1)
                stage_T(k + 1)
            if 0 <= k < 16:
                stage_PV(k)
        P.op("dve", lambda e: e.memset(self.pgv(9, 1, BF16)[:, 0:2], 0.0), reads=r_Pm, writes=self.prs(9) + r_Pm)
        P.op("dve", lambda e: e.memset(self.pgv(10, 1, BF16)[:, 0:2], 0.0), reads=r_PT, writes=self.prs(10) + r_PT)
        self.fm_proj_res(self.w_o[l], lambda k, tt: self.aview(k, tt * 512, (tt + 1) * 512),
                         lambda k: [self.rA[k]], accumulate_sq=bool(self.cfg.ffn or (self.cfg.mixer and l + 1 < self.cfg.depth)))

    def mixer(self, b, l):
        P = self.P
        cfg = self.cfg
        sm = self.small.ap()
        self.rmsnorm_fm("g_mix", l)
        gvbc = self.pgv(4)[:, 0:512]
        bsbc = self.pgv(4)[:, 512:1024]
        wsT = self.pgv(5, 1, BF16)[:, 0:512]
        P.dma("sp", lambda e: [e.dma_start(out=gvbc, in_=self.gv[l:l + 1, :].partition_broadcast(128)),
                               e.dma_start(out=bsbc, in_=self.bs[l:l + 1, :].partition_broadcast(128))],
              "d_mx0", writes=self.prs(4), n=2)
        P.dma("pool", lambda e: e.dma_start(out=wsT, in_=self.wsT[l]), "d_mx1", writes=self.prs(5))
        zr = lambda k, tt: self.aview(k, tt * 512, (tt + 1) * 512)
        zreg = lambda k: [self.rA[k]]

        def evac_ug(m, pv, prs):
            P.op("act", lambda e: e.activation(out=self.bview(4 + m), in_=pv, func=AF.Gelu_apprx_tanh),
                 reads=list(prs), writes=[self.rB[4 + m]])

        def evac_us(m, pv, prs):
            P.op("act", lambda e: e.copy(out=self.bview(m), in_=pv), reads=list(prs), writes=[self.rB[m]])
        self.fm_proj(self.w_in[l], 512, 4, zr, zreg, evac_ug, kouter_first=True)
        self.fm_proj(self.w_in[l], 0, 4, zr, zreg, evac_us)
        pans = []
        for half in range(2):
            slot, rslot, sk = self.ring_next()
            sv = slot.ap().rearrange("p (k m) -> p k m", k=8)
            src = self.w_in[l][:, 1024 + half * 256: 1024 + (half + 1) * 256].rearrange("(k p) m -> p k m", p=128)
            P.dma("pool", (lambda sv=sv, src=src: lambda e: e.dma_start(out=sv, in_=src))(), sk, writes=[rslot])
            pans.append((sv, rslot))
        if not hasattr(self, "r_vp"):
            self.r_vp = {nm: [Reg(nm + "0"), Reg(nm + "1")] for nm in ("vt", "vn", "tmp", "st")}
        rv = self.r_vp
        guard_pages = self.prs(6) + self.prs(7) + self.prs(8)
        allv = [r for nm in ("vt", "vn", "tmp") for r in rv[nm]]
        P.op("dve", lambda e: e.memset(self.pgv(6)[:, 0:1], 0.0), writes=guard_pages + allv)

        def vbufs(n):
            par = n % 2
            return dict(par=par, vb=par, sbk=2 + par,
                        vt=self.pgv(6)[:, par * 512:(par + 1) * 512],
                        vn=self.pgv(7, 1, BF16)[:, par * 512:(par + 1) * 512],
                        tmp=self.pgv(8)[:, par * 512:(par + 1) * 512],
                        st=sm[:, 64 + par * 4: 64 + par * 4 + 4])

        def V12(n):
            bf = vbufs(n)
            par, vb, vt, tmp, st = bf["par"], bf["vb"], bf["vt"], bf["tmp"], bf["st"]
            for half in range(2):
                sv, rslot = pans[half]
                for k in range(8):
                    self.mm(self.psv(vb * 512 + half * 256, vb * 512 + (half + 1) * 256),
                            self.aview(k, n * 128, (n + 1) * 128), sv[:, k, :], k == 0, k == 7,
                            [rslot, self.rA[k]], [self.rps[vb]], last=(k == 7))
            vps = self.psv(vb * 512, (vb + 1) * 512)
            P.op("act", lambda e: e.activation(out=vt, in_=vps, func=AF.Gelu_apprx_tanh),
                 reads=[self.rps[vb]], writes=[rv["vt"][par]])
            P.op("dve", lambda e: e.scalar_tensor_tensor(out=tmp, in0=vt, scalar=1.0, in1=vt, op0=ALU.mult, op1=ALU.mult,
                                                         accum_out=st[:, 0:1]),
                 reads=[rv["vt"][par]], writes=[rv["tmp"][par], rv["st"][par]])
            P.op("act", lambda e: e.activation(out=st[:, 1:2], in_=st[:, 0:1], func=AF.Ln, scale=1.0 / 512.0, bias=self.eps_ap),
                 reads=[rv["st"][par], self.r_small], writes=[rv["st"][par]])
            P.op("act", lambda e: e.activation(out=st[:, 1:2], in_=st[:, 1:2], func=AF.Exp, scale=-0.5),
                 reads=[rv["st"][par]], writes=[rv["st"][par]])

        def V34(n):
            bf = vbufs(n)
            par, sbk, vt, vn, st = bf["par"], bf["sbk"], bf["vt"], bf["vn"], bf["st"]
            P.op("dve", lambda e: e.scalar_tensor_tensor(out=vn, in0=vt, scalar=st[:, 1:2], in1=gvbc, op0=ALU.mult, op1=ALU.mult),
                 reads=[rv["vt"][par], rv["st"][par]] + self.prs(4), writes=[rv["vn"][par]])
            for h in range(4):
                self.mm(self.psv(sbk * 512 + h * 128, sbk * 512 + (h + 1) * 128), vn[:, h * 128:(h + 1) * 128],
                        wsT[:, h * 128:(h + 1) * 128], True, True, [rv["vn"][par]] + self.prs(5), [self.rps[sbk]], last=(h == 3))

        def V5(n):
            bf = vbufs(n)
            par, sbk, tmp = bf["par"], bf["sbk"], bf["tmp"]
            P.op("dve", lambda e: e.tensor_tensor(out=tmp, in0=self.psv(sbk * 512, (sbk + 1) * 512), in1=bsbc, op=ALU.add),
                 reads=[self.rps[sbk]] + self.prs(4), writes=[rv["tmp"][par]])
            ugv = cap(self.B.ap(), 4 * SEQ + n * 128, [[SEQ, 4], [1, 128]])
            tmp3 = tmp.rearrange("p (h q) -> p h q", h=4)
            P.op("dve", lambda e: e.tensor_tensor(out=ugv, in0=tmp3, in1=ugv, op=ALU.mult),
                 reads=[rv["tmp"][par]] + self.rB[4:8], writes=self.rB[4:8])

        for k in range(-2, 16):
            if 0 <= k + 2 < 16:
                V12(k + 2)
            if 0 <= k + 1 < 16:
                V34(k + 1)
            if 0 <= k < 16:
                V5(k)
        P.op("dve", lambda e: e.memset(self.pgv(6)[:, 0:1], 0.0), reads=allv, writes=guard_pages + allv)
        if cfg.mixer == "g":
            for m in range(4):
                P.op("dve", (lambda m=m: lambda e: e.memset(self.aview(m), 0.0))(), writes=[self.rA[m]])
        else:
            self.s5(b, l)
        yr = lambda k, tt: (self.aview(k, tt * 512, (tt + 1) * 512) if k < 4 else self.bview(k, tt * 512, (tt + 1) * 512))
        yreg = lambda k: [self.rA[k]] if k < 4 else [self.rB[k]]
        if cfg.xattn:
            self.xattn_kv(b, l)
        self.fm_proj_res(self.w_out[l], yr, yreg, accumulate_sq=bool(cfg.xattn or cfg.ffn))

    def range_reduce(self, X, rX, TF, rTF, pre_add=0.0):
        P = self.P
        PI = math.pi
        C1 = 6.28125
        C2 = 2 * PI - C1
        TI = TF.bitcast(mybir.dt.int32)
        if pre_add != 0.0:
            P.op("dve", lambda e: e.tensor_scalar(out=X, in0=X, scalar1=pre_add, scalar2=None, op0=ALU.add), reads=rX, writes=rX)
        P.op("dve", lambda e: e.tensor_scalar(out=TF, in0=X, scalar1=1.0 / (2 * PI), scalar2=None, op0=ALU.mult), reads=rX, writes=rTF)
        P.op("dve", lambda e: e.tensor_copy(out=TI, in_=TF), reads=rTF, writes=rTF)
        P.op("dve", lambda e: e.tensor_copy(out=TF, in_=TI), reads=rTF, writes=rTF)
        P.op("dve", lambda e: e.scalar_tensor_tensor(out=X, in0=TF, scalar=-C1, in1=X, op0=ALU.mult, op1=ALU.add), reads=rTF + rX, writes=rX)
        P.op("dve", lambda e: e.scalar_tensor_tensor(out=X, in0=TF, scalar=-C2, in1=X, op0=ALU.mult, op1=ALU.add), reads=rTF + rX, writes=rX)
        P.op("dve", lambda e: e.tensor_scalar(out=TF, in0=X, scalar1=PI, scalar2=None, op0=ALU.is_gt), reads=rX, writes=rTF)
        P.op("dve", lambda e: e.scalar_tensor_tensor(out=X, in0=TF, scalar=-2 * PI, in1=X, op0=ALU.mult, op1=ALU.add), reads=rTF + rX, writes=rX)
        P.op("dve", lambda e: e.tensor_scalar(out=TF, in0=X, scalar1=-PI, scalar2=None, op0=ALU.is_lt), reads=rX, writes=rTF)
        P.op("dve", lambda e: e.scalar_tensor_tensor(out=X, in0=TF, scalar=2 * PI, in1=X, op0=ALU.mult, op1=ALU.add), reads=rTF + rX, writes=rX)
        P.op("dve", lambda e: e.tensor_scalar(out=X, in0=X, scalar1=-PI, scalar2=PI, op0=ALU.max, op1=ALU.min), reads=rX, writes=rX)

    def s5_prologue(self, l):
        P = self.P
        TT = lambda out, in0, in1, op, reads, writes, eng="dve": P.op(
            eng, lambda e: e.tensor_tensor(out=out, in0=in0, in1=in1, op=op), reads=reads, writes=writes)
        pA, rA_ = self.pgv(0), self.prs(0)
        pBm, rBm = self.pgv(1), self.prs(1)
        pCm, rCm = self.pgv(2), self.prs(2)
        pK, rK = self.pgv(13), self.prs(13)
        MAGK, rMAG = self.pgv(3), self.prs(3)
        Li, rLi = self.pgv(4), self.prs(4)
        Lr, rLr = self.pgv(5), self.prs(5)
        Bb, rBb = self.pgv(6), self.prs(6)
        self.r_s5w = getattr(self, "r_s5w", {})
        self.r_tab = getattr(self, "r_tab", {})
        PI = math.pi
        P.dma("sp", lambda e: [e.dma_start(out=pA[:, 0:96], in_=self.s5p[l]), e.dma_start(out=pBm, in_=self.s5b[l]),
                               e.dma_start(out=pCm, in_=self.s5c[l]), e.dma_start(out=pK, in_=self.s5k)],
              "d_s5in", writes=rA_ + rBm + rCm + rK, n=4)
        ar, ai, ldt = pA[:, 0:32], pA[:, 32:64], pA[:, 64:96]
        dt, adt, ang, den = pA[:, 96:128], pA[:, 128:160], pA[:, 160:192], pA[:, 192:224]
        fre, fim, ta, tb, ang8 = pA[:, 224:256], pA[:, 256:288], pA[:, 288:320], pA[:, 320:352], pA[:, 352:384]
        tc_, rden = pA[:, 384:416], pA[:, 416:448]
        KV = pK[:, 0:26]
        sgn = pK[:, 26:27]
        cidx = pK[:, 32:288]
        jmask = pK[:, 288:544]
        maskF = pK[:, 544:672]
        maskB = pK[:, 672:800]
        rw = rA_
        P.op("act", lambda e: e.activation(out=dt, in_=ldt, func=AF.Exp), reads=rw, writes=rw)
        TT(adt, ar, dt, ALU.mult, rw, rw)
        TT(ang, ai, dt, ALU.mult, rw, rw)
        b3 = lambda v: cap(v, 0, [[1, 32], [0, 26]])
        k3 = cap(KV, 0, [[0, 32], [1, 26]])
        M3 = MAGK[:, 0:832].rearrange("p (g k) -> p g k", g=32)
        Li3 = Li[:, 0:832].rearrange("p (g k) -> p g k", g=32)
        Lr3 = Lr[:, 0:832].rearrange("p (g k) -> p g k", g=32)
        TT(M3, b3(adt), k3, ALU.mult, rw + rK, rMAG)
        P.op("act", lambda e: e.activation(out=MAGK[:, 0:832], in_=MAGK[:, 0:832], func=AF.Exp), reads=rMAG, writes=rMAG)
        TT(Li3, b3(ang), k3, ALU.mult, rw + rK, rLi)
        TF7, rTF7 = self.pgv(7), self.prs(7)
        P.op("dve", lambda e: e.tensor_copy(out=Lr[:, 0:832], in_=Li[:, 0:832]), reads=rLi, writes=rLr)
        self.range_reduce(Lr[:, 0:832], rLr, TF7[:, 0:832], rTF7, pre_add=PI / 2)
        self.range_reduce(Li[:, 0:832], rLi, TF7[:, 0:832], rTF7)
        for (T_, rT) in ((Lr, rLr), (Li, rLi)):
            P.op("act", (lambda T_=T_: lambda e: e.activation(out=T_[:, 0:832], in_=T_[:, 0:832], func=AF.Sin))(), reads=rT, writes=rT)
            TT(T_[:, 0:832], T_[:, 0:832], MAGK[:, 0:832], ALU.mult, rT + rMAG, rT)
        kcol = lambda T_, k: cap(T_, k, [[26, 32]])
        Lr1, Li1 = kcol(Lr, 25), kcol(Li, 25)
        TT(den, ar, ar, ALU.mult, rw, rw)
        TT(ta, ai, ai, ALU.mult, rw, rw)
        TT(den, den, ta, ALU.add, rw, rw)
        P.op("dve", lambda e: e.reciprocal(out=rden, in_=den), reads=rw, writes=rw)
        P.op("dve", lambda e: e.tensor_scalar(out=tc_, in0=Lr1, scalar1=-1.0, scalar2=None, op0=ALU.add), reads=rLr, writes=rw)
        TT(ta, tc_, ar, ALU.mult, rw, rw)
        TT(tb, Li1, ai, ALU.mult, rw + rLi, rw)
        TT(fre, ta, tb, ALU.add, rw, rw)
        TT(fre, fre, rden, ALU.mult, rw, rw)
        TT(ta, Li1, ar, ALU.mult, rw + rLi, rw)
        TT(tb, tc_, ai, ALU.mult, rw, rw)
        TT(fim, ta, tb, ALU.subtract, rw, rw)
        TT(fim, fim, rden, ALU.mult, rw, rw)
        P.op("dve", lambda e: e.tensor_scalar(out=ang8, in0=ang, scalar1=8.0, scalar2=None, op0=ALU.mult), reads=rw, writes=rw)
        T1, rT1 = self.pgv(7), self.prs(7)
        T2, rT2 = self.pgv(8), self.prs(8)
        g16 = lambda v: cap(v, 0, [[1, 32], [0, 16]])
        v3 = lambda v: v.rearrange("p (g h) -> p g h", g=32)
        Bre, Bim = pBm[:, 0:512], pBm[:, 512:1024]
        Bbr, Bbi = Bb[:, 0:512], Bb[:, 512:1024]
        TT(v3(T1[:, 0:512]), v3(Bre), g16(fre), ALU.mult, rBm + rw, rT1)
        TT(v3(T1[:, 512:1024]), v3(Bim), g16(fim), ALU.mult, rBm + rw, rT1)
        TT(Bbr, T1[:, 0:512], T1[:, 512:1024], ALU.subtract, rT1, rBb)
        TT(v3(T1[:, 0:512]), v3(Bim), g16(fre), ALU.mult, rBm + rw, rT1)
        TT(v3(T1[:, 512:1024]), v3(Bre), g16(fim), ALU.mult, rBm + rw, rT1)
        TT(Bbi, T1[:, 0:512], T1[:, 512:1024], ALU.add, rT1, rBb)
        Cre, Cim = pCm[:, 0:512], pCm[:, 512:1024]
        stg_n = [0]

        def stage_out(src, rsrc, dram_ap, rdram, eng="act"):
            i = stg_n[0] % 2
            stg_n[0] += 1
            stg = self.pgv(12, 1, BF16)[:, i * 1024:(i + 1) * 1024]
            rstg = self.prs(12)
            if eng == "act":
                P.op("act", lambda e: e.copy(out=stg, in_=src), reads=rsrc, writes=rstg)
            else:
                P.op("dve", lambda e: e.tensor_copy(out=stg, in_=src), reads=rsrc, writes=rstg)
            P.dma("sp", lambda e: e.dma_start(out=dram_ap, in_=stg), "d_stg%d" % i, reads=rstg, writes=[rdram])

        for q in range(4):
            regs = [Reg("s5w_%d_%d_%d" % (l, q, i)) for i in range(5)]
            self.r_s5w[(l, q)] = regs
            def Lv(T_, k0):
                return cap(T_, q * 8 * 26 + k0, [[26, 8], [1, 8], [0, 16]])

            def Xv(T_):
                return cap(T_, q * 128, [[16, 8], [0, 8], [1, 16]])
            o4 = lambda T_: T_.rearrange("p (g s h) -> p g s h", g=8, s=8)
            WA, rWA = self.pgv(9), self.prs(9)
            WB, rWB = self.pgv(10), self.prs(10)
            WY, rWY = self.pgv(11), self.prs(11)
            for (k0, sec, keep) in ((0, 0, False), (8, None, True)):
                TT(o4(WA), Lv(Lr, k0), Xv(Bbr), ALU.mult, rLr + rBb, rWA)
                TT(o4(T1), Lv(Li, k0), Xv(Bbi), ALU.mult, rLi + rBb, rT1)
                TT(WA, WA, T1, ALU.subtract, rWA + rT1, rWA)
                TP, rTP = self.pgv(12), self.prs(12)
                TT(o4(WB), Lv(Lr, k0), Xv(Bbi), ALU.mult, rLr + rBb, rWB, "dve")
                TT(o4(TP), Lv(Li, k0), Xv(Bbr), ALU.mult, rLi + rBb, rTP, "dve")
                TT(WB, WB, TP, ALU.add, rWB + rTP, rWB, "dve")
                if not keep:
                    for wi, (W_, rW_) in enumerate(((WA, rWA), (WB, rWB))):
                        for hb in range(2):
                            bank = hb
                            for gi in range(4):
                                g = hb * 4 + gi
                                o = self.psv(bank * 512 + gi * 128, bank * 512 + (gi + 1) * 128)
                                i_ = W_[:, g * 128:(g + 1) * 128]
                                P.op("pe", (lambda o=o, i_=i_: lambda e: e.transpose(out=o, in_=i_, identity=self.ident.ap()))(),
                                     reads=rW_ + [self.r_ident], writes=[self.rps[bank]], milestone=(gi == 3), skip_self=True)
                            P.op("act", (lambda bank=bank, hb=hb: lambda e: e.copy(
                                out=T2.bitcast(BF16)[:, hb * 512:(hb + 1) * 512], in_=self.psv(bank * 512, (bank + 1) * 512)))(),
                                reads=[self.rps[bank]], writes=rT2)
                        P.dma("sp", (lambda wi=wi, q=q: lambda e: e.dma_start(
                            out=self.s5w_d[l, q, :, wi * 1024:(wi + 1) * 1024], in_=T2.bitcast(BF16)[:, 0:1024]))(),
                            "d_stgT", reads=rT2, writes=[regs[wi]])
            def Cv(T_):
                return cap(T_, q * 128, [[16, 8], [0, 8], [1, 16]])

            def Ly(T_):
                return cap(T_, q * 8 * 26 + 16, [[26, 8], [1, 8], [0, 16]])
            WYi, rWYi = self.pgv(3), self.prs(3)
            WYi, rWYi = T2, rT2
            TT(o4(WY), Cv(Cre), Ly(Lr), ALU.mult, rCm + rLr, rWY)
            TT(o4(T1), Cv(Cim), Ly(Li), ALU.mult, rCm + rLi, rT1)
            TT(WY, WY, T1, ALU.subtract, rWY + rT1, rWY)
            TP, rTP = self.pgv(12), self.prs(12)
            TT(o4(WYi), Cv(Cre), Ly(Li), ALU.mult, rCm + rLi, rWYi, "dve")
            TT(o4(TP), Cv(Cim), Ly(Lr), ALU.mult, rCm + rLr, rTP, "dve")
            TT(WYi, WYi, TP, ALU.add, rWYi + rTP, rWYi, "dve")
            P.op("act", lambda e: e.mul(out=WYi, in_=WYi, mul=-1.0), reads=rWYi, writes=rWYi)
            stage_out(WY, rWY, self.s5w_d[l, q, :, 2048:3072], regs[2])
            stage_out(WYi, rWYi, self.s5w_d[l, q, :, 3072:4096], regs[3])
            mY, rmY = self.pgv(12), self.prs(12)
            mYi, rmYi = T1, rT1
            for d_ in range(2):
                rm = pK[:, 800 + d_: 801 + d_]
                P.op("act", (lambda rm=rm: lambda e: e.mul(out=mY, in_=WY, mul=rm))(),
                     reads=rWY + rK, writes=rmY)
                P.op("act", (lambda rm=rm: lambda e: e.mul(out=mYi, in_=WYi, mul=rm))(),
                     reads=rWYi + rK, writes=rmYi)
                for g in range(8):
                    bank = 2 + 2 * d_ + g // 4
                    o = self.psv(bank * 512 + (g % 4) * 128, bank * 512 + (g % 4 + 1) * 128)
                    self.mm(o, WA[:, g * 128:(g + 1) * 128], mY[:, g * 128:(g + 1) * 128], True, False,
                            rWA + rmY, [self.rps[bank]], last=False)
                    self.mm(o, WB[:, g * 128:(g + 1) * 128], mYi[:, g * 128:(g + 1) * 128], False, True,
                            rWB + rmYi, [self.rps[bank]], last=True)
            mF = cap(maskF, 0, [[0, 4], [1, 128]])
            mB = cap(maskB, 0, [[0, 4], [1, 128]])
            for hb in range(2):
                Mh3 = T1[:, hb * 512:(hb + 1) * 512].rearrange("p (g m) -> p g m", g=4)
                pf = self.psv((2 + hb) * 512, (3 + hb) * 512).rearrange("p (g m) -> p g m", g=4)
                pb_ = self.psv((4 + hb) * 512, (5 + hb) * 512).rearrange("p (g m) -> p g m", g=4)
                TT(Mh3, pf, mF, ALU.mult, [self.rps[2 + hb]] + rK, rT1)
                P.op("dve", (lambda pb_=pb_: lambda e: e.tensor_tensor(out=pb_, in0=pb_, in1=mB, op=ALU.mult))(),
                     reads=[self.rps[4 + hb]] + rK, writes=[self.rps[4 + hb]])
                TT(Mh3, Mh3, pb_, ALU.add, rT1 + [self.rps[4 + hb]], rT1)
            stage_out(T1, rT1, self.s5w_d[l, q, :, 4096:5120], regs[4], eng="dve")
        kidx = pK[:, 832:864]
        for q in range(4):
            CS, rCS = self.pgv(9, 2), self.prs(9, 2)
            SN, rSN = self.pgv(7, 2), self.prs(7, 2)
            RH, rRH = self.pgv(11, 2), self.prs(11, 2)
            SM, rSM = self.pgv(6), self.prs(6)
            sC, sS, sT = SM[:, 0:256], SM[:, 256:512], SM[:, 512:768]
            s3 = lambda T_: T_.rearrange("p (g k) -> p g k", g=8)
            a8 = cap(ang8, q * 8, [[1, 8], [0, 32]])
            kk = cap(kidx, 0, [[0, 8], [1, 32]])
            TT(s3(sS), a8, kk, ALU.mult, rw + rK, rSM)
            P.op("dve", lambda e: e.tensor_copy(out=sC, in_=sS), reads=rSM, writes=rSM)
            self.range_reduce(sC, rSM, sT, rSM, pre_add=PI / 2)
            self.range_reduce(sS, rSM, sT, rSM)
            P.op("act", lambda e: e.activation(out=SM[:, 0:512], in_=SM[:, 0:512], func=AF.Sin), reads=rSM, writes=rSM)
            P.op("dve", lambda e: e.tensor_scalar(out=sS, in0=sS, scalar1=sgn, scalar2=None, op0=ALU.mult), reads=rSM + rK, writes=rSM)
            vA = lambda T_: cap(T_, 0, [[32, 8], [1, 16], [0, 16]])
            vB = lambda T_: cap(T_, 16, [[32, 8], [0, 16], [1, 16]])
            o4 = lambda T_: T_.rearrange("p (g a b) -> p g a b", g=8, a=16)
            TT(o4(CS), vA(sC), vB(sC), ALU.mult, rSM, rCS)
            TT(o4(RH), vA(sS), vB(sS), ALU.mult, rSM, rRH)
            TT(CS, CS, RH, ALU.subtract, rCS + rRH, rCS)
            TQ, rTQ = self.pgv(1, 2), self.prs(1, 2)
            TT(o4(SN), vA(sS), vB(sC), ALU.mult, rSM, rSN, "dve")
            TT(o4(TQ), vA(sC), vB(sS), ALU.mult, rSM, rTQ, "dve")
            TT(SN, SN, TQ, ALU.add, rSN + rTQ, rSN, "dve")
            t3 = lambda T_: T_.rearrange("p (g j) -> p g j", g=8)
            m8 = cap(MAGK, q * 8 * 26 + 24, [[26, 8], [0, 256]])
            jm = cap(jmask, 0, [[0, 8], [1, 256]])
            TT(t3(RH), m8, jm, ALU.mult, rMAG + rK, rRH)
            CSb, rCSb = self.pgv(4, 1, BF16), self.prs(4)
            SNb, rSNb = self.pgv(5, 1, BF16), self.prs(5)
            P.op("act", lambda e: e.copy(out=CSb, in_=CS), reads=rCS, writes=rCSb)
            P.op("act", lambda e: e.copy(out=SNb, in_=SN), reads=rSN, writes=rSNb)
            for hb in range(2):
                rg = Reg("tab_%d_%d" % (l, q * 2 + hb))
                self.r_tab[(l, q * 2 + hb)] = rg
                dstb = self.tabb_d[l, q * 2 + hb]
                dstr = self.rho_d[l, q * 2 + hb]
                P.dma("sp", (lambda dstb=dstb, dstr=dstr, hb=hb: lambda e: [
                    e.dma_start(out=dstb[:, 0:1024], in_=CSb[:, hb * 1024:(hb + 1) * 1024]),
                    e.dma_start(out=dstb[:, 1024:2048], in_=SNb[:, hb * 1024:(hb + 1) * 1024]),
                    e.dma_start(out=dstr, in_=RH[:, hb * 1024:(hb + 1) * 1024])])(),
                    "d_tabw", reads=rCSb + rSNb + rRH, writes=[rg], n=3)

    def s5(self, b, l):
        P = self.P
        TT = lambda out, in0, in1, op, reads, writes: P.op(
            "dve", lambda e: e.tensor_tensor(out=out, in0=in0, in1=in1, op=op), reads=reads, writes=writes)
        LZ, rLZ = self.pgv(0, 1, BF16), self.prs(0)
        WYM, rWYM = self.pgv(1, 2, BF16), self.prs(1, 2)
        TABS = [self.pgv(3, 3), self.A.ap()[:, 4 * SEQ:7 * SEQ].bitcast(F32)]
        rTABS = [self.prs(3, 3), self.rA[4:7]]
        UQ = [self.pgv(6, 1, BF16), self.aview(7)]
        rUQ = [self.prs(6), [self.rA[7]]]
        Yq, rYq = self.pgv(7, 1, BF16), self.prs(7)
        T1, rT1 = self.pgv(8), self.prs(8)
        T2, rT2 = self.pgv(9), self.prs(9)
        Av, rAv = self.pgv(10), self.prs(10)
        Bv, rBv = self.pgv(11), self.prs(11)
        XSS = [self.pgv(12, 1, BF16), self.pgv(13, 1, BF16)]
        rXSS = [self.prs(12), self.prs(13)]
        selb = self.selb.ap()
        ZRE, ZIM = self.psv(2 * 512, 4 * 512), self.psv(4 * 512, 6 * 512)
        rZRE, rZIM = self.rps[2:4], self.rps[4:6]
        LZre = lambda g: LZ[:, g * 128:(g + 1) * 128]
        LZim = lambda g: LZ[:, 1024 + g * 128: 1024 + (g + 1) * 128]
        WYr = lambda g: WYM[:, g * 128:(g + 1) * 128]
        WYi = lambda g: WYM[:, 1024 + g * 128: 1024 + (g + 1) * 128]
        Mg = lambda g: WYM[:, 2048 + g * 128: 2048 + (g + 1) * 128]

        def load_LZ(q):
            P.dma("sp", lambda e: e.dma_start(out=LZ, in_=self.s5w_d[l, q, :, 0:2048]), "d_w5a",
                  reads=self.r_s5w[(l, q)][0:2], writes=rLZ)

        def load_WYM(q):
            P.dma("sp", lambda e: e.dma_start(out=WYM[:, 0:3072], in_=self.s5w_d[l, q, :, 2048:5120]), "d_w5b",
                  reads=self.r_s5w[(l, q)][2:5], writes=rWYM)

        def load_TAB(q, half):
            hb = 2 * q + half
            tb16 = TABS[half].bitcast(BF16)
            P.dma("sp", lambda e: [e.dma_start(out=tb16[:, 0:2048], in_=self.tabb_d[l, hb]),
                                   e.dma_start(out=TABS[half][:, 1024:2048], in_=self.rho_d[l, hb])], "d_tab%d" % half,
                  reads=[self.r_tab[(l, hb)]], writes=rTABS[half], n=2)

        def SEL(q):
            Uq, rUq = UQ[q % 2], rUQ[q % 2]
            for g8 in range(8):
                bank = 6 + (g8 // 2) % 2
                o = self.psv(bank * 512 + (g8 % 2) * 256, bank * 512 + (g8 % 2 + 1) * 256)
                for s0 in range(8):
                    self.mm(o, selb[:, g8 * 240 + (7 - s0) * 16: g8 * 240 + (7 - s0) * 16 + 128],
                            self.bview(q)[:, s0:SEQ:8], s0 == 0, s0 == 7, [self.r_selb, self.rB[q]], [self.rps[bank]],
                            last=(s0 == 7))
                if g8 % 2 == 1:
                    P.op("act", (lambda bank=bank, g8=g8: lambda e: e.copy(
                        out=Uq[:, (g8 - 1) * 256:(g8 + 1) * 256], in_=self.psv(bank * 512, (bank + 1) * 512)))(),
                        reads=[self.rps[bank]], writes=rUq)

        def Zmm(q, half):
            Uq, rUq = UQ[q % 2], rUQ[q % 2]
            for gi in range(4):
                g8 = half * 4 + gi
                U = Uq[:, g8 * 256:(g8 + 1) * 256]
                for (Z, rZ, LZf) in ((ZRE, rZRE, LZre), (ZIM, rZIM, LZim)):
                    self.mm(Z[:, gi * 256:(gi + 1) * 256], LZf(g8), U, True, True, rLZ + rUq, [rZ[gi // 2]], last=True)

        def Bst(q, half):
            TAB, rTAB = TABS[half], rTABS[half]
            XS, rXS = XSS[half], rXSS[half]
            tb16 = TAB.bitcast(BF16)
            COS, SIN, RHO = tb16[:, 0:1024], tb16[:, 1024:2048], TAB[:, 1024:2048]
            b16 = lambda T_: T_.bitcast(BF16)[:, 0:1024]
            T1b, T2b, Avb, Bvb = b16(T1), b16(T2), b16(Av), b16(Bv)
            TT(T1b, ZRE, COS, ALU.mult, rZRE + rTAB, rT1)
            TT(T2b, ZIM, SIN, ALU.mult, rZIM + rTAB, rT2)
            TT(Avb, T1b, T2b, ALU.add, rT1 + rT2, rAv)
            TT(T1b, ZIM, COS, ALU.mult, rZIM + rTAB, rT1)
            TT(T2b, ZRE, SIN, ALU.mult, rZRE + rTAB, rT2)
            TT(Bvb, T1b, T2b, ALU.subtract, rT1 + rT2, rBv)
            Wr, Wi, rWr, rWi = T1b, T2b, rT1, rT2
            f_ = lambda T_: T_[0:64, :]
            r_ = lambda T_: cap(T_[64:128, :], 1023, [[-1, 1024]])
            for (W_, rW_, S_, rS_) in ((Wr, rWr, Avb, rAv), (Wi, rWi, Bvb, rBv)):
                P.op("dve", (lambda W_=W_, S_=S_: lambda e: e.tensor_tensor_scan(
                    out=f_(W_), data0=f_(RHO), data1=f_(S_), initial=0.0, op0=ALU.mult, op1=ALU.add))(),
                    reads=rTAB + rS_, writes=rW_)
                P.op("dve", (lambda W_=W_, S_=S_: lambda e: e.tensor_tensor_scan(
                    out=r_(W_), data0=r_(RHO), data1=r_(S_), initial=0.0, op0=ALU.mult, op1=ALU.add))(),
                    reads=rTAB + rS_, writes=rW_)
            P1, P2, rP1, rP2 = Avb, Bvb, rAv, rBv
            fo = lambda off: cap(XS[0:64, :], off + 1, [[256, 4], [1, 255]])
            fi = lambda T_: cap(T_[0:64, :], 0, [[256, 4], [1, 255]])
            bo = lambda off: cap(XS[64:128, :], off, [[256, 4], [1, 255]])
            bi = lambda T_: cap(T_[64:128, :], 1, [[256, 4], [1, 255]])
            TT(P1, Wr, COS, ALU.mult, rWr + rTAB, rP1)
            TT(P2, Wi, SIN, ALU.mult, rWi + rTAB, rP2)
            P.op("dve", lambda e: e.memset(cap(XS[0:64, :], 0, [[256, 8], [1, 1]]), 0.0), writes=rXS)
            P.op("dve", lambda e: e.memset(cap(XS[64:128, :], 255, [[256, 8], [1, 1]]), 0.0), writes=rXS)
            TT(fo(0), fi(P1), fi(P2), ALU.subtract, rP1 + rP2, rXS)
            TT(bo(0), bi(P1), bi(P2), ALU.subtract, rP1 + rP2, rXS)
            TT(P1, Wr, SIN, ALU.mult, rWr + rTAB, rP1)
            TT(P2, Wi, COS, ALU.mult, rWi + rTAB, rP2)
            TT(fo(1024), fi(P1), fi(P2), ALU.add, rP1 + rP2, rXS)
            TT(bo(1024), bi(P1), bi(P2), ALU.add, rP1 + rP2, rXS)

        def Ymm(q, half):
            Uq, rUq = UQ[q % 2], rUQ[q % 2]
            XS, rXS = XSS[half], rXSS[half]
            for gi in range(4):
                g8 = half * 4 + gi
                bank = 6 + gi // 2
                o = self.psv(bank * 512 + (gi % 2) * 256, bank * 512 + (gi % 2 + 1) * 256)
                U = Uq[:, g8 * 256:(g8 + 1) * 256]
                rr = rWYM + rUq + rXS
                self.mm(o, Mg(g8), U, True, False, rr, [self.rps[bank]], last=False)
                self.mm(o, WYr(g8), XS[:, gi * 256:(gi + 1) * 256], False, False, rr, [self.rps[bank]], last=False)
                self.mm(o, WYi(g8), XS[:, 1024 + gi * 256: 1024 + (gi + 1) * 256], False, True, rr, [self.rps[bank]], last=True)
                if gi % 2 == 1:
                    P.op("act", (lambda bank=bank, g8=g8: lambda e: e.copy(
                        out=Yq[:, (g8 - 1) * 256:(g8 + 1) * 256], in_=self.psv(bank * 512, (bank + 1) * 512)))(),
                        reads=[self.rps[bank]], writes=rYq)

        def UNSEL(q, part):
            for t0 in range(part * 4, part * 4 + 4):
                tb = (t0 % 4) // 2
                o = self.psv(tb * 512 + (t0 % 2) * 256, tb * 512 + (t0 % 2 + 1) * 256)
                for g8 in range(8):
                    self.mm(o, selb[:, t0 * 240 + (7 - g8) * 16: t0 * 240 + (7 - g8) * 16 + 128],
                            Yq[:, g8 * 256:(g8 + 1) * 256], g8 == 0, g8 == 7,
                            [self.r_selb] + rYq, [self.rps[tb]], last=(g8 == 7))

        def POST(q, part):
            TMP, rTMP = (Av, rAv) if part == 0 else (Bv, rBv)
            dcol = self.col("ssm_d", l, q)
            uperm = cap(self.bview(q), part * 4, [[1, 4], [8, 256]])
            tm3 = TMP.rearrange("p (t c) -> p t c", t=4)
            ps3 = self.psv(0, 1024).rearrange("p (t c) -> p t c", t=4)
            P.op("dve", lambda e: e.scalar_tensor_tensor(out=tm3, in0=uperm, scalar=dcol, in1=ps3, op0=ALU.mult, op1=ALU.add),
                 reads=[self.rB[q], self.r_colp] + self.rps[0:2], writes=rTMP)
            P.op("act", lambda e: e.activation(out=uperm, in_=tm3, func=AF.Gelu_apprx_tanh),
                 reads=rTMP, writes=[self.rB[q]])

        load_LZ(0)
        load_WYM(0)
        load_TAB(0, 0)
        SEL(0)
        Zmm(0, 0)
        for q in range(4):
            load_TAB(q, 1)
            Bst(q, 0)
            Zmm(q, 1)
            if q < 3:
                load_LZ(q + 1)
            if q >= 1:
                UNSEL(q - 1, 0)
                POST(q - 1, 0)
            if q < 3:
                SEL(q + 1)
            if q >= 1:
                UNSEL(q - 1, 1)
            Ymm(q, 0)
            if q < 3:
                load_TAB(q + 1, 0)
            Bst(q, 1)
            if q >= 1:
                POST(q - 1, 1)
            if q < 3:
                Zmm(q + 1, 0)
            Ymm(q, 1)
            if q < 3:
                load_WYM(q + 1)
        UNSEL(3, 0)
        POST(3, 0)
        UNSEL(3, 1)
        POST(3, 1)
        SG, rSG = self.pgv(8, 1, BF16), self.prs(8)

        def evac_glu(m, pv, prs):
            bcol = self.col("b_glu", l, m)
            P.op("act", lambda e: e.activation(out=SG, in_=pv, func=AF.Sigmoid, bias=bcol),
                 reads=list(prs) + [self.r_colp], writes=rSG)
            P.op("dve", lambda e: e.tensor_tensor(out=self.aview(m), in0=self.bview(m), in1=SG, op=ALU.mult),
                 reads=rSG + [self.rB[m]], writes=[self.rA[m]])
        self.fm_proj(self.w_glu[l], 0, 4, lambda k, tt: self.bview(k, tt * 512, (tt + 1) * 512),
                     lambda k: [self.rB[k]], evac_glu, nk=4)

    def final_out(self, b):
        P = self.P
        fg_bc = self.pgv(13)
        r_fg = self.prs(13)
        P.dma("sp", lambda e: e.dma_start(out=fg_bc, in_=self.final_g.partition_broadcast(128)), "d_c2", writes=r_fg)
        for n in range(16):
            ot = self.pgv(11 + (n % 2))
            rot = self.rpg[11 + (n % 2)]
            ss = self.small.ap()[:, 8 + 4 * (n % 2): 8 + 4 * (n % 2) + 4]
            banks = [(2 * n) % 8, (2 * n + 1) % 8]
            for half in range(2):
                bank = banks[half]
                for cc in range(4):
                    c = half * 4 + cc
                    o = self.psv(bank * 512 + cc * 128, bank * 512 + (cc + 1) * 128)
                    i_ = self.hview(c, n * 128, (n + 1) * 128)
                    P.op("pe", (lambda o=o, i_=i_: lambda e: e.transpose(out=o, in_=i_, identity=self.ident.ap()))(),
                         reads=[self.rH[c], self.r_ident], writes=[self.rps[bank]], milestone=(cc == 3), skip_self=True)
                P.op("act", (lambda bank=bank, half=half, ot=ot, ss=ss: lambda e: e.activation(
                    out=ot[:, half * 512:(half + 1) * 512], in_=self.psv(bank * 512, (bank + 1) * 512),
                    func=AF.Square, accum_out=ss[:, half:half + 1]))(),
                    reads=[self.rps[bank]], writes=[rot, self.r_small])
            P.op("dve", (lambda ss=ss: lambda e: e.tensor_tensor(out=ss[:, 2:3], in0=ss[:, 0:1], in1=ss[:, 1:2], op=ALU.add))(),
                 reads=[self.r_small], writes=[self.r_small])
            P.op("act", (lambda ss=ss: lambda e: e.activation(out=ss[:, 3:4], in_=ss[:, 2:3], func=AF.Ln,
                                                              scale=1.0 / D_MODEL, bias=self.eps_ap))(),
                 reads=[self.r_small], writes=[self.r_small])
            P.op("act", (lambda ss=ss: lambda e: e.activation(out=ss[:, 3:4], in_=ss[:, 3:4], func=AF.Exp, scale=-0.5))(),
                 reads=[self.r_small], writes=[self.r_small])
            for half in range(2):
                bank = banks[half]
                P.op("dve", (lambda bank=bank, half=half, ot=ot, ss=ss: lambda e: e.scalar_tensor_tensor(
                    out=ot[:, half * 512:(half + 1) * 512], in0=self.psv(bank * 512, (bank + 1) * 512),
                    scalar=ss[:, 3:4], in1=fg_bc[:, half * 512:(half + 1) * 512],
                    op0=ALU.mult, op1=ALU.mult))(),
                    reads=[self.rps[bank], self.r_small, rot] + r_fg, writes=[rot])
            P.dma("sp", (lambda ot=ot, n=n: lambda e: e.dma_start(out=self.out[b, n * 128:(n + 1) * 128, :], in_=ot))(),
                  "d_out%d" % (n % 2), reads=[rot], writes=[self.r_out])

    def build(self):
        cfg, P = self.cfg, self.P
        self.r_out = Reg("out")
        self.pb_n = 0
        self.pb2_n = 0
        self.sq_ready = False
        self.consts()
        P.op("dve", lambda e: e.memset(self.small.ap(), 0.0), writes=[self.r_small])
        P.op("dve", lambda e: e.memset(self.small.ap()[:, 0:1], EPS), writes=[self.r_small])
        self.eps_ap = self.small.ap()[:, 0:1]
        if cfg.mixer is True or "s5" in cfg.dumps:
            for l in range(cfg.depth):
                self.s5_prologue(l)
        for b in range(cfg.nseq):
            self.load_x(b)
            for l in range(cfg.depth):
                if cfg.mixer:
                    self.mixer(b, l)
                if cfg.xattn:
                    self.xattn(b, l)
                if cfg.ffn:
                    self.ffn(l)
            self.final_out(b)
        P.wait_all("sp", [self.r_out] + [r for rs in getattr(self, "r_s5w", {}).values() for r in rs] + list(getattr(self, "r_tab", {}).values()))
        P.emit()
        P.close()
        print("[kernel] ops per engine:", {e: len(P.ops[e]) for e in ENGS}, "sbuf left", self.nc.sbuf_bytes_remaining)


def host_prep(inputs, cfg):
    d = cfg.depth
    off, ncol = colp_layout(d)
    colp = np.zeros((128, ncol), np.float32)

    def put(nm, l, vec):
        v = np.asarray(vec, np.float32).reshape(-1, 128).T
        colp[:, off[(nm, l)]: off[(nm, l)] + v.shape[1]] = v
    for l in range(d):
        put("g_mix", l, inputs["norm_mix_g"][l])
        put("g_x", l, inputs["norm_xattn_g"][l])
        put("g_f", l, inputs["norm_ffn_g"][l])
        put("ssm_d", l, inputs["ssm_d"][l])
        put("b_glu", l, inputs["b_glu"][l])
        put("cw0", l, inputs["conv_w"][l, 0])
        put("cw1", l, inputs["conv_w"][l, 1])
        put("cw2", l, inputs["conv_w"][l, 2])
        put("cb", l, inputs["conv_b"][l])
    shared = {
        "colp": colp,
        "final_g": np.ascontiguousarray(np.asarray(inputs["final_g"], np.float32).reshape(1, D_MODEL)),
        "ident": np.eye(128, dtype=np.float32),
        "w_up": np.ascontiguousarray(np.asarray(inputs["w_up"], np.float32)[:d]),
        "w_q": np.ascontiguousarray(np.asarray(inputs["w_q"], np.float32)[:d]),
        "w_in": np.ascontiguousarray(np.asarray(inputs["w_in"], np.float32)[:d]),
        "w_out": np.ascontiguousarray(np.asarray(inputs["w_out"], np.float32)[:d]),
        "gv": np.ascontiguousarray(np.asarray(inputs["gmlp_norm_g"], np.float32)[:d]),
        "bs": np.ascontiguousarray(np.asarray(inputs["gmlp_b_s"], np.float32)[:d].reshape(d, 512)),
        "wsT": np.ascontiguousarray(np.asarray(inputs["gmlp_w_s"], np.float32)[:d].transpose(0, 3, 1, 2).reshape(d, 128, 512)),
        "w_kv": np.ascontiguousarray(np.asarray(inputs["w_kv"], np.float32)[:d]),
        "w_o": np.ascontiguousarray(np.asarray(inputs["w_o"], np.float32)[:d]),
        "mem_g": np.ascontiguousarray(np.asarray(inputs["mem_norm_g"], np.float32)[:d]),
        "w_down": np.ascontiguousarray(np.asarray(inputs["w_down"], np.float32)[:d]),
    }
    tr = lambda a, perm: np.asarray(a, np.float32)[:d].transpose(perm)
    a_re = tr(inputs["ssm_a_re"], (0, 1, 3, 2)).reshape(d, 128, 32)
    a_im = tr(inputs["ssm_a_im"], (0, 1, 3, 2)).reshape(d, 128, 32)
    ldt = np.repeat(np.asarray(inputs["ssm_log_dt"], np.float32)[:d, :, None, :], 64, axis=2).reshape(d, 128, 32)
    shared["s5p"] = np.ascontiguousarray(np.concatenate([a_re, a_im, ldt], axis=2))
    b_re = tr(inputs["ssm_b_re"], (0, 1, 3, 2, 4)).reshape(d, 128, 512)
    b_im = tr(inputs["ssm_b_im"], (0, 1, 3, 2, 4)).reshape(d, 128, 512)
    shared["s5b"] = np.ascontiguousarray(np.concatenate([b_re, b_im], axis=2))
    c_re = tr(inputs["ssm_c_re"], (0, 1, 4, 2, 3)).reshape(d, 128, 512)
    c_im = tr(inputs["ssm_c_im"], (0, 1, 4, 2, 3)).reshape(d, 128, 512)
    shared["s5c"] = np.ascontiguousarray(np.concatenate([c_re, c_im], axis=2))
    shared["w_glu"] = np.ascontiguousarray(np.asarray(inputs["w_glu"], np.float32)[:d])
    k = np.zeros((128, 1024), np.float32)
    j = np.arange(8, dtype=np.float32)
    k[:64, 0:8] = 7 - j; k[64:, 0:8] = j
    k[:64, 8:16] = -1 - j; k[64:, 8:16] = j - 8
    k[:64, 16:24] = j + 1; k[64:, 16:24] = 8 - j
    k[:, 24] = 8.0; k[:, 25] = 1.0
    k[:64, 26] = 1.0; k[64:, 26] = -1.0
    jj = np.arange(256, dtype=np.float32)
    k[:, 32:288] = jj
    k[:, 288:544] = 1.0; k[:64, 288] = 0.0; k[64:, 543] = 0.0
    k[:64, 800] = 1.0; k[64:, 801] = 1.0
    k[:, 832:848] = 16.0 * np.arange(16, dtype=np.float32); k[:, 848:864] = np.arange(16, dtype=np.float32)
    sidx = np.arange(128) // 16
    k[:, 544:672] = (sidx[:, None] <= sidx[None, :]).astype(np.float32)
    k[:, 672:800] = (sidx[:, None] >= sidx[None, :]).astype(np.float32)
    shared["s5k"] = k
    selb = np.zeros((128, 8, 240), np.float32)
    for a in range(8):
        for h in range(16):
            selb[a * 16 + h, a, 112 + h] = 1.0
    shared["selb"] = selb.reshape(128, 8 * 240)
    return shared


def run(inputs, cfg, ncores=NCORES):
    nc = bass.Bass("TRN2", target_bir_lowering=False)
    bld = Builder(nc, cfg)
    bld.build()
    shared = host_prep(inputs, cfg)
    x = np.asarray(inputs["x"], np.float32)
    mem = np.asarray(inputs["mem"], np.float32)
    in_maps = []
    for c in range(ncores):
        m = dict(shared)
        m["x"] = np.ascontiguousarray(x[c * cfg.nseq:(c + 1) * cfg.nseq])
        m["mem"] = np.ascontiguousarray(mem[c * cfg.nseq:(c + 1) * cfg.nseq])
        in_maps.append(m)
    res = run_bass_kernel_spmd(nc, in_maps, core_ids=list(range(ncores)))
    return res


def kernel(**inputs):
    cfg = Cfg()
    res = run(inputs, cfg)
    out = np.concatenate([np.asarray(r["out"], np.float32) for r in res.results], axis=0)
    return out
```
